# Optimizing a Trainium2 kernel written in Bass

```python
import math
import jax, jax.numpy as jnp
from jax import lax
import numpy as np


D_MODEL = 1024
BATCH = 8
SEQ = 4096
DEPTH = 2

N_EVEN = (DEPTH + 1) // 2
N_ODD = DEPTH // 2

A_HEADS = 8
A_HEAD_DIM = D_MODEL // 16
A_WIDTH = A_HEADS * A_HEAD_DIM
MOBA_BLOCK = 256
MOBA_TOPK = 3
MOBA_QCHUNK = 32
REL_BUCKETS = 32
REL_MAX_DIST = 128

B_HEADS = 4
B_HEAD_DIM = D_MODEL // 8
B_WIDTH = B_HEADS * B_HEAD_DIM
MLSTM_CONV = 4
MLSTM_CHUNK = 64

POOL_WINDOWS = (2, 4, 8, 16)
POOL_GROUPS = 4
POOL_GROUP_DIM = D_MODEL // 8
POOL_WIDTH = POOL_GROUPS * POOL_GROUP_DIM

R_HEADS = 4
R_QK_DIM = D_MODEL // 16
R_V_DIM = 2 * R_QK_DIM
R_QK_WIDTH = R_HEADS * R_QK_DIM
R_V_WIDTH = R_HEADS * R_V_DIM
RET_CHUNK = 64
ROPE_BASE = 10000.0

D_FF = 2816
FFN_CONV = 3

EPS = 1e-6
NEG = -1e30

AB_SIZES = (A_WIDTH, A_WIDTH, A_WIDTH, 2 * B_WIDTH, B_WIDTH, B_WIDTH, B_HEADS, B_HEADS)
CD_SIZES = (POOL_WIDTH, R_QK_WIDTH, R_QK_WIDTH, R_V_WIDTH, R_V_WIDTH)
IN_AB = 3 * A_WIDTH + 4 * B_WIDTH + 2 * B_HEADS
IN_CD = POOL_WIDTH + 2 * R_QK_WIDTH + 2 * R_V_WIDTH
MIX_AB = A_WIDTH + B_WIDTH
MIX_CD = POOL_WIDTH + R_V_WIDTH

kernel_name = 'hybrid_moba_mlstm_pool_retention'


def rmsnorm(x, g):
    xf = x.astype(jnp.float32)
    y = xf * lax.rsqrt(jnp.mean(xf * xf, axis=-1, keepdims=True) + EPS)
    return (y * g.astype(jnp.float32)).astype(x.dtype)


def head_rmsnorm(t, g):
    n_heads = t.shape[1]
    tf = t.astype(jnp.float32)
    y = tf * lax.rsqrt(jnp.mean(tf * tf, axis=-1, keepdims=True) + EPS)
    return y * g.astype(jnp.float32).reshape(n_heads, 1, -1)


def split_cols(z, sizes):
    idx = []
    acc = 0
    for s in sizes[:-1]:
        acc += s
        idx.append(acc)
    return jnp.split(z, idx, axis=-1)


def to_heads(t, n_heads):
    b, s, w = t.shape
    return t.reshape(b, s, n_heads, w // n_heads).transpose(0, 2, 1, 3)


def from_heads(t):
    b, h, s, d = t.shape
    return t.transpose(0, 2, 1, 3).reshape(b, s, h * d)


def causal_dwconv(x, w, b):
    k = w.shape[0]
    c = x.shape[-1]
    y = lax.conv_general_dilated(x, w[:, None, :].astype(x.dtype), window_strides=(1,), padding=[(k - 1, 0)],
                                 dimension_numbers=('NWC', 'WIO', 'NWC'), feature_group_count=c)
    return y + b.astype(x.dtype)


def rel_bucket(dist):
    max_exact = REL_BUCKETS // 2
    d = jnp.maximum(dist, 0)
    df = jnp.maximum(d, 1).astype(jnp.float32)
    large = max_exact + (jnp.log(df / max_exact) / math.log(REL_MAX_DIST / max_exact)
                         * (REL_BUCKETS - max_exact)).astype(jnp.int32)
    large = jnp.minimum(large, REL_BUCKETS - 1)
    return jnp.where(d < max_exact, d, large)


def moba_attention(q, k, v, rel_bias):
    bsz, n_h, s, dh = q.shape
    nb = -(-s // MOBA_BLOCK)
    pad = nb * MOBA_BLOCK - s
    kb = jnp.pad(k, ((0, 0), (0, 0), (0, pad), (0, 0))).reshape(bsz, n_h, nb, MOBA_BLOCK, dh)
    vb = jnp.pad(v, ((0, 0), (0, 0), (0, pad), (0, 0))).reshape(bsz, n_h, nb, MOBA_BLOCK, dh)
    kmean = jnp.mean(kb.astype(jnp.float32), axis=3)
    n_sel = min(MOBA_TOPK, nb - 1)
    n_chunks = s // MOBA_QCHUNK
    scale = dh ** -0.5
    bias_tab = rel_bias.astype(jnp.float32).T
    blk_off = jnp.arange(MOBA_BLOCK)
    b_idx = jnp.arange(bsz)[:, None, None, None]
    h_idx = jnp.arange(n_h)[None, :, None, None]
    h_idx5 = jnp.arange(n_h)[None, :, None, None, None]

    def chunk(c):
        q0 = c * MOBA_QCHUNK
        qc = lax.dynamic_slice_in_dim(q, q0, MOBA_QCHUNK, axis=2).astype(jnp.float32)
        qpos = q0 + jnp.arange(MOBA_QCHUNK)
        cur = q0 // MOBA_BLOCK
        k_own = lax.dynamic_index_in_dim(kb, cur, axis=2, keepdims=False).astype(jnp.float32)
        v_own = lax.dynamic_index_in_dim(vb, cur, axis=2, keepdims=False).astype(jnp.float32)
        dist_own = qpos[:, None] - (cur * MOBA_BLOCK + blk_off)[None, :]
        s_own = jnp.einsum('bhqd,bhkd->bhqk', qc, k_own) * scale + bias_tab[:, rel_bucket(dist_own)]
        s_own = jnp.where(dist_own >= 0, s_own, NEG)
        if n_sel == 0:
            p_own = jax.nn.softmax(s_own, axis=-1)
            return jnp.einsum('bhqk,bhkd->bhqd', p_own, v_own).astype(q.dtype)
        gate = jnp.einsum('bhqd,bhnd->bhqn', qc, kmean)
        gate = jnp.where(jnp.arange(nb) < cur, gate, NEG)
        _, idx = lax.top_k(gate, n_sel)
        valid = jnp.arange(n_sel) < cur
        k_sel = kb[b_idx, h_idx, idx].astype(jnp.float32)
        v_sel = vb[b_idx, h_idx, idx].astype(jnp.float32)
        dist_sel = qpos[None, None, :, None, None] - (idx[..., None] * MOBA_BLOCK + blk_off)
        s_sel = jnp.einsum('bhqd,bhqnkd->bhqnk', qc, k_sel) * scale + bias_tab[h_idx5, rel_bucket(dist_sel)]
        s_sel = jnp.where(valid[:, None], s_sel, NEG).reshape(bsz, n_h, MOBA_QCHUNK, n_sel * MOBA_BLOCK)
        p = jax.nn.softmax(jnp.concatenate([s_sel, s_own], axis=-1), axis=-1)
        p_sel = p[..., :n_sel * MOBA_BLOCK].reshape(bsz, n_h, MOBA_QCHUNK, n_sel, MOBA_BLOCK)
        p_own = p[..., n_sel * MOBA_BLOCK:]
        out = jnp.einsum('bhqnk,bhqnkd->bhqd', p_sel, v_sel) + jnp.einsum('bhqk,bhkd->bhqd', p_own, v_own)
        return out.astype(q.dtype)

    outs = lax.map(chunk, jnp.arange(n_chunks))
    return outs.transpose(1, 2, 0, 3, 4).reshape(bsz, n_h, s, dh)


def mlstm(q, k, v, i_pre, f_pre):
    bsz, n_h, s, dh = q.shape
    L = MLSTM_CHUNK
    nc = s // L
    def chunks(a):
        return jnp.moveaxis(a.reshape((bsz, n_h, nc, L) + a.shape[3:]), 2, 0)
    qf = chunks(q.astype(jnp.float32))
    kf = chunks(k.astype(jnp.float32) * (dh ** -0.5))
    vf = chunks(v.astype(jnp.float32))
    ig = chunks(i_pre.astype(jnp.float32))
    lf = chunks(jax.nn.log_sigmoid(f_pre.astype(jnp.float32)))
    causal = jnp.tril(jnp.ones((L, L), dtype=bool))

    def step(carry, inp):
        C, n, m = carry
        qc, kc, vc, ic, fc = inp
        bcum = jnp.cumsum(fc, axis=-1)
        log_inter = bcum + m[..., None]
        log_intra = jnp.where(causal, bcum[..., :, None] - bcum[..., None, :] + ic[..., None, :], -jnp.inf)
        m_t = jnp.maximum(log_inter, jnp.max(log_intra, axis=-1))
        w_inter = jnp.exp(log_inter - m_t)
        qk = jnp.einsum('bhtd,bhsd->bhts', qc, kc) * jnp.exp(log_intra - m_t[..., None])
        num = w_inter[..., None] * jnp.einsum('bhvk,bhtk->bhtv', C, qc) + jnp.einsum('bhts,bhsv->bhtv', qk, vc)
        den = w_inter * jnp.einsum('bhk,bhtk->bht', n, qc) + jnp.sum(qk, axis=-1)
        h = num / jnp.maximum(jnp.abs(den), jnp.exp(-m_t))[..., None]
        b_last = bcum[..., -1]
        log_s = b_last[..., None] - bcum + ic
        m_new = jnp.maximum(b_last + m, jnp.max(log_s, axis=-1))
        w_prev = jnp.exp(b_last + m - m_new)
        w_s = jnp.exp(log_s - m_new[..., None])
        C_new = w_prev[..., None, None] * C + jnp.einsum('bhsv,bhsk->bhvk', vc * w_s[..., None], kc)
        n_new = w_prev[..., None] * n + jnp.einsum('bhs,bhsk->bhk', w_s, kc)
        return (C_new, n_new, m_new), h

    init = (jnp.zeros((bsz, n_h, dh, dh), jnp.float32), jnp.zeros((bsz, n_h, dh), jnp.float32),
            jnp.zeros((bsz, n_h), jnp.float32))
    _, hs = lax.scan(step, init, (qf, kf, vf, ig, lf))
    return jnp.moveaxis(hs, 0, 2).reshape(bsz, n_h, s, dh)


def rope(x):
    s, d = x.shape[2], x.shape[3]
    half = d // 2
    inv = ROPE_BASE ** (-jnp.arange(half, dtype=jnp.float32) / half)
    ang = jnp.arange(s, dtype=jnp.float32)[:, None] * inv[None, :]
    cos, sin = jnp.cos(ang), jnp.sin(ang)
    x1, x2 = x[..., :half], x[..., half:]
    return jnp.concatenate([x1 * cos - x2 * sin, x1 * sin + x2 * cos], axis=-1)


def retention(q, k, v):
    bsz, n_h, s, dk = q.shape
    dv = v.shape[-1]
    L = RET_CHUNK
    nc = s // L
    log_g = jnp.log(1.0 - jnp.exp2(-5.0 - jnp.arange(n_h, dtype=jnp.float32)))
    t = jnp.arange(L, dtype=jnp.float32)
    rel = t[:, None] - t[None, :]
    decay_intra = jnp.where(rel >= 0, jnp.exp(jnp.maximum(rel, 0.0) * log_g[:, None, None]), 0.0)
    decay_q = jnp.exp((t + 1.0) * log_g[:, None])
    decay_k = jnp.exp((L - 1.0 - t) * log_g[:, None])
    decay_chunk = jnp.exp(L * log_g)
    def chunks(a):
        return jnp.moveaxis(a.reshape(bsz, n_h, nc, L, a.shape[-1]), 2, 0)

    def step(R, inp):
        qc, kc, vc = inp
        intra = jnp.einsum('bhts,bhsv->bhtv', jnp.einsum('bhtd,bhsd->bhts', qc, kc) * decay_intra, vc)
        inter = jnp.einsum('bhtd,bhdv->bhtv', qc * decay_q[..., None], R)
        R_new = decay_chunk[:, None, None] * R + jnp.einsum('bhsd,bhsv->bhdv', kc * decay_k[..., None], vc)
        return R_new, intra + inter

    R0 = jnp.zeros((bsz, n_h, dk, dv), jnp.float32)
    _, ys = lax.scan(step, R0, (chunks(q), chunks(k), chunks(v.astype(jnp.float32))))
    return jnp.moveaxis(ys, 0, 2).reshape(bsz, n_h, s, dv)


def multiscale_pool(u, pool_w, pool_scale):
    bsz, s, _ = u.shape
    ug = u.astype(jnp.float32).reshape(bsz, s, POOL_GROUPS, POOL_GROUP_DIM)
    cs = jnp.concatenate([jnp.zeros((bsz, 1, POOL_GROUPS, POOL_GROUP_DIM), jnp.float32),
                          jnp.cumsum(ug, axis=1)], axis=1)
    pos = jnp.arange(s)
    outs = []
    for g, w in enumerate(POOL_WINDOWS):
        start = jnp.maximum(pos + 1 - w, 0)
        cnt = (pos + 1 - start).astype(jnp.float32)
        mean = (cs[:, 1:, g] - cs[:, start, g]) / cnt[None, :, None]
        outs.append(mean - ug[:, :, g])
    pooled = jnp.stack(outs, axis=2)
    mixed = jnp.einsum('bsgc,gcd->bsgd', pooled, pool_w.astype(jnp.float32))
    return mixed.reshape(bsz, s, POOL_WIDTH) * pool_scale.astype(jnp.float32)


def mixer_ab(y, w_in, w_out, conv_w, conv_b, b_i, b_f, mlstm_g, rel_bias):
    z = y @ w_in
    aq, ak, av, bqk, bv, bo, bi, bf = split_cols(z, AB_SIZES)
    ya = moba_attention(to_heads(aq, A_HEADS), to_heads(ak, A_HEADS), to_heads(av, A_HEADS), rel_bias)
    bqk = jax.nn.silu(causal_dwconv(bqk, conv_w, conv_b))
    bq, bk = jnp.split(bqk, 2, axis=-1)
    ig = (bi + b_i).transpose(0, 2, 1)
    fg = (bf + b_f).transpose(0, 2, 1)
    hb = mlstm(to_heads(bq, B_HEADS), to_heads(bk, B_HEADS), to_heads(bv, B_HEADS), ig, fg)
    yb = from_heads(head_rmsnorm(hb, mlstm_g)) * jax.nn.sigmoid(bo.astype(jnp.float32))
    cat = jnp.concatenate([from_heads(ya).astype(y.dtype), yb.astype(y.dtype)], axis=-1)
    return cat @ w_out


def mixer_cd(y, w_in, w_out, pool_w, pool_scale, ret_g):
    z = y @ w_in
    pu, rq, rk, rv, rg = split_cols(z, CD_SIZES)
    yc = multiscale_pool(pu, pool_w, pool_scale)
    q = rope(to_heads(rq, R_HEADS).astype(jnp.float32))
    k = rope(to_heads(rk, R_HEADS).astype(jnp.float32)) * (R_QK_DIM ** -0.5)
    hd = retention(q, k, to_heads(rv, R_HEADS))
    yd = from_heads(head_rmsnorm(hd, ret_g)) * jax.nn.silu(rg.astype(jnp.float32))
    cat = jnp.concatenate([yc.astype(y.dtype), yd.astype(y.dtype)], axis=-1)
    return cat @ w_out


def conv_ffn(y, w_up, conv_w, conv_b, w_down):
    u = causal_dwconv(y @ w_up, conv_w, conv_b)
    gate, val = jnp.split(u, 2, axis=-1)
    return (jax.nn.silu(gate) * val) @ w_down


def setup_inputs(seed: int = 0) -> dict:
    key = jax.random.key(seed)
    ks = jax.random.split(key, 24)
    f32 = jnp.float32
    def nrm(k, shape, scale):
        return jax.random.normal(k, shape, f32) * scale
    x = nrm(ks[0], (BATCH, SEQ, D_MODEL), 1.0)
    rel_bias = nrm(ks[1], (REL_BUCKETS, A_HEADS), 0.5)
    norm_mix_g = 1.0 + nrm(ks[2], (DEPTH, D_MODEL), 0.02)
    norm_ffn_g = 1.0 + nrm(ks[3], (DEPTH, D_MODEL), 0.02)
    norm_out_g = 1.0 + nrm(ks[4], (D_MODEL,), 0.02)
    w_in_ab = nrm(ks[5], (N_EVEN, D_MODEL, IN_AB), D_MODEL ** -0.5)
    w_out_ab = nrm(ks[6], (N_EVEN, MIX_AB, D_MODEL), MIX_AB ** -0.5)
    mlstm_conv_w = nrm(ks[7], (N_EVEN, MLSTM_CONV, 2 * B_WIDTH), MLSTM_CONV ** -0.5)
    mlstm_conv_b = nrm(ks[8], (N_EVEN, 2 * B_WIDTH), 0.01)
    mlstm_b_i = nrm(ks[9], (N_EVEN, B_HEADS), 0.1)
    mlstm_b_f = jnp.linspace(3.0, 6.0, B_HEADS, dtype=f32)[None, :] + nrm(ks[10], (N_EVEN, B_HEADS), 0.01)
    mlstm_norm_g = 1.0 + nrm(ks[11], (N_EVEN, B_WIDTH), 0.02)
    w_in_cd = nrm(ks[12], (N_ODD, D_MODEL, IN_CD), D_MODEL ** -0.5)
    w_out_cd = nrm(ks[13], (N_ODD, MIX_CD, D_MODEL), MIX_CD ** -0.5)
    pool_w = nrm(ks[14], (N_ODD, POOL_GROUPS, POOL_GROUP_DIM, POOL_GROUP_DIM), POOL_GROUP_DIM ** -0.5)
    pool_scale = 1.0 + nrm(ks[15], (N_ODD, POOL_WIDTH), 0.02)
    ret_norm_g = 1.0 + nrm(ks[16], (N_ODD, R_V_WIDTH), 0.02)
    ffn_w_up = nrm(ks[17], (DEPTH, D_MODEL, 2 * D_FF), D_MODEL ** -0.5)
    ffn_conv_w = nrm(ks[18], (DEPTH, FFN_CONV, 2 * D_FF), FFN_CONV ** -0.5)
    ffn_conv_b = nrm(ks[19], (DEPTH, 2 * D_FF), 0.01)
    ffn_w_down = nrm(ks[20], (DEPTH, D_FF, D_MODEL), D_FF ** -0.5)
    return {'x': x, 'rel_bias': rel_bias, 'norm_mix_g': norm_mix_g, 'norm_ffn_g': norm_ffn_g,
            'norm_out_g': norm_out_g, 'w_in_ab': w_in_ab, 'w_out_ab': w_out_ab,
            'mlstm_conv_w': mlstm_conv_w, 'mlstm_conv_b': mlstm_conv_b, 'mlstm_b_i': mlstm_b_i,
            'mlstm_b_f': mlstm_b_f, 'mlstm_norm_g': mlstm_norm_g, 'w_in_cd': w_in_cd, 'w_out_cd': w_out_cd,
            'pool_w': pool_w, 'pool_scale': pool_scale, 'ret_norm_g': ret_norm_g, 'ffn_w_up': ffn_w_up,
            'ffn_conv_w': ffn_conv_w, 'ffn_conv_b': ffn_conv_b, 'ffn_w_down': ffn_w_down}


def reference(x, rel_bias, norm_mix_g, norm_ffn_g, norm_out_g, w_in_ab, w_out_ab, mlstm_conv_w, mlstm_conv_b,
              mlstm_b_i, mlstm_b_f, mlstm_norm_g, w_in_cd, w_out_cd, pool_w, pool_scale, ret_norm_g,
              ffn_w_up, ffn_conv_w, ffn_conv_b, ffn_w_down):
    h = x
    for layer in range(DEPTH):
        j = layer // 2
        y = rmsnorm(h, norm_mix_g[layer])
        if layer % 2 == 0:
            mix = mixer_ab(y, w_in_ab[j], w_out_ab[j], mlstm_conv_w[j], mlstm_conv_b[j], mlstm_b_i[j],
                           mlstm_b_f[j], mlstm_norm_g[j], rel_bias)
        else:
            mix = mixer_cd(y, w_in_cd[j], w_out_cd[j], pool_w[j], pool_scale[j], ret_norm_g[j])
        h = h + mix.astype(h.dtype)
        f = conv_ffn(rmsnorm(h, norm_ffn_g[layer]), ffn_w_up[layer], ffn_conv_w[layer], ffn_conv_b[layer],
                     ffn_w_down[layer])
        h = h + f.astype(h.dtype)
    return rmsnorm(h, norm_out_g)
```

```python
import contextlib
import math
import numpy as np
import concourse.bass as bass
import concourse.mybir as mybir
from concourse.bass_utils import run_bass_kernel_spmd

F32 = mybir.dt.float32
BF16 = mybir.dt.bfloat16
AF = mybir.ActivationFunctionType
ALU = mybir.AluOpType
AX = mybir.AxisListType

N_DMA_SEMS = 40
S_LEN = 4096
D = 1024
NG = 8
GT = 512
EPS = 1e-6
PARTS = set('nqkvbog')
ARRIVAL_ORDER = True
BIG = 30000.0


class T:
    def __init__(self, ap, name=""):
        self.ap = ap
        self.name = name
        self.last_w = []
        self.readers = []

    def __getitem__(self, idx):
        return V(self, self.ap[idx])

    @property
    def v(self):
        return V(self, self.ap)

    def sub(self, idx, name=""):
        return T(self.ap[idx], name or self.name)


class V:
    def __init__(self, t, ap):
        self.t = t
        self.ap = ap

    def __getitem__(self, idx):
        return V(self.t, self.ap[idx])


def _ap(x):
    return x.ap if isinstance(x, (V, T)) else x


class Sched:
    ENGS = ("pe", "act", "dve", "pool", "sp")

    def __init__(self, nc, stack):
        self.nc = nc
        self.q = {e: [] for e in self.ENGS}
        self.cnt = {e: 0 for e in self.ENGS}
        self.seen = {e: {} for e in self.ENGS}
        self.dcnt = [0] * N_DMA_SEMS
        self.dnext = 0
        self.semkey = {}
        for e in self.ENGS:
            self.semkey[("e", e)] = stack.enter_context(nc.semaphore("s_" + e))
        for i in range(N_DMA_SEMS):
            self.semkey[("d", i)] = stack.enter_context(nc.semaphore("d%d" % i))
        self.rr = 0

    def _deps(self, reads, writes):
        deps = []
        for t in reads:
            deps += t.last_w
        for t in writes:
            deps += t.last_w
            deps += t.readers
        return deps

    def _commit(self, reads, writes, ticket):
        for t in reads:
            t.readers.append(ticket)
            if len(t.readers) > 48:
                best = {}
                for k, v in t.readers:
                    if best.get(k, -1) < v:
                        best[k] = v
                t.readers = list(best.items())
        for t in writes:
            best = dict(t.last_w)
            if best.get(ticket[0], -1) < ticket[1]:
                best[ticket[0]] = ticket[1]
            t.last_w = list(best.items())
            t.readers = []

    def _waits(self, eng, deps, skip_self=False):
        best = {}
        for k, v in deps:
            if skip_self and k == ("e", eng):
                continue
            if best.get(k, -1) < v:
                best[k] = v
        out = []
        seen = self.seen[eng]
        for k, v in best.items():
            if seen.get(k, -1) >= v:
                continue
            seen[k] = v
            out.append((k, v))
        return out

    def op(self, eng, fn, reads=(), writes=(), inc=True, skip_self=False):
        reads = [r.t if isinstance(r, V) else r for r in reads if isinstance(r, (V, T))]
        writes = [w.t if isinstance(w, V) else w for w in writes if isinstance(w, (V, T))]
        waits = self._waits(eng, self._deps(reads, writes), skip_self=skip_self)
        if inc:
            self.cnt[eng] += 1
            ticket = (("e", eng), self.cnt[eng])
        else:
            ticket = (("e", eng), self.cnt[eng] + 1)
        self.q[eng].append((waits, fn, (("e", eng), 1) if inc else None))
        self._commit(reads, writes, ticket)
        return ticket

    def dma(self, eng, out, in_, after=(), **kw):
        r = [in_.t] if isinstance(in_, V) else []
        w = [out.t] if isinstance(out, V) else []
        i = self.dnext
        self.dnext = (self.dnext + 1) % N_DMA_SEMS
        deps = self._deps(r, w) + list(after)
        if self.dcnt[i] > 0:
            deps.append((("d", i), self.dcnt[i]))
        waits = self._waits(eng, deps)
        self.dcnt[i] += 16
        ticket = (("d", i), self.dcnt[i])
        o, s = _ap(out), _ap(in_)
        self.q[eng].append((waits, (lambda e, o=o, s=s, kw=kw: e.dma_start(out=o, in_=s, **kw)), (("d", i), 16)))
        self._commit(r, w, ticket)
        return ticket

    def barrier(self):
        deps = [(("e", e), self.cnt[e]) for e in self.ENGS if self.cnt[e] > 0]
        deps += [(("d", i), self.dcnt[i]) for i in range(N_DMA_SEMS) if self.dcnt[i] > 0]
        for e in self.ENGS:
            waits = self._waits(e, deps)
            if waits:
                self.q[e].append((waits, None, None))

    def matmul(self, out, lhsT, rhs, start=True, stop=True, inc=None):
        o, l, r = _ap(out), _ap(lhsT), _ap(rhs)
        if inc is None:
            inc = stop
        fn = lambda e: e.matmul(o, l, r, start=start, stop=stop)
        if start:
            t = self.op("pe", fn, reads=[lhsT, rhs], writes=[out], inc=inc, skip_self=True)
        else:
            t = self.op("pe", fn, reads=[lhsT, rhs], writes=[], inc=inc, skip_self=True)
        if not stop:
            t = (t[0], self.cnt["pe"] + 1) if not inc else t
        best = dict(out.t.last_w)
        if best.get(t[0], -1) < t[1]:
            best[t[0]] = t[1]
        out.t.last_w = list(best.items())
        return t

    def transpose(self, out, in_, ident):
        o, i, d = _ap(out), _ap(in_), _ap(ident)
        return self.op("pe", lambda e: e.transpose(o, i, d), reads=[in_, ident], writes=[out], skip_self=True)

    def act(self, out, in_, func, bias=None, scale=None):
        o, i = _ap(out), _ap(in_)
        kw = {}
        if bias is not None:
            kw["bias"] = _ap(bias)
        if scale is not None:
            kw["scale"] = _ap(scale)
        return self.op("act", lambda e: e.activation(o, i, func, **kw), reads=[in_, bias, scale], writes=[out])

    def tt(self, eng, out, in0, in1, op):
        o, a, b = _ap(out), _ap(in0), _ap(in1)
        return self.op(eng, lambda e: e.tensor_tensor(o, a, b, op), reads=[in0, in1], writes=[out])

    def ts(self, eng, out, in0, s1, s2, op0, op1=None):
        o, a, x1, x2 = _ap(out), _ap(in0), _ap(s1), _ap(s2)
        if op1 is None:
            fn = lambda e: e.tensor_scalar(o, a, x1, None, op0)
        else:
            fn = lambda e: e.tensor_scalar(o, a, x1, x2, op0, op1)
        return self.op(eng, fn, reads=[in0, s1, s2], writes=[out])

    def stt(self, eng, out, in0, scalar, in1, op0, op1):
        o, a, sc, b = _ap(out), _ap(in0), _ap(scalar), _ap(in1)
        return self.op(eng, lambda e: e.scalar_tensor_tensor(o, a, sc, b, op0, op1), reads=[in0, in1, scalar], writes=[out])

    def copy(self, eng, out, in_):
        o, i = _ap(out), _ap(in_)
        if eng == "act":
            return self.op(eng, lambda e: e.copy(o, i), reads=[in_], writes=[out])
        if eng == "dve":
            return self.op(eng, lambda e: e.tensor_scalar(o, i, 1.0, None, ALU.mult), reads=[in_], writes=[out])
        return self.op(eng, lambda e: e.tensor_copy(o, i), reads=[in_], writes=[out])

    def memset(self, eng, out, val):
        o = _ap(out)
        return self.op(eng, lambda e: e.memset(o, val), reads=[], writes=[out])

    def recip(self, out, in_):
        o, i = _ap(out), _ap(in_)
        return self.op("dve", lambda e: e.reciprocal(o, i), reads=[in_], writes=[out])

    def evac(self, out, in_, scale=None):
        self.rr ^= 1
        if self.rr:
            if scale is None:
                return self.copy("act", out, in_)
            return self.act(out, in_, AF.Copy, scale=scale)
        if scale is None:
            return self.copy("dve", out, in_)
        return self.ts("dve", out, in_, scale, None, ALU.mult)

    def emit(self):
        nc = self.nc
        engmap = {"pe": "tensor", "act": "scalar", "dve": "vector", "pool": "gpsimd", "sp": "sync"}
        with nc.Block() as block:
            for e in self.ENGS:
                q = self.q[e]
                if not q:
                    continue

                def body(engine, q=q):
                    for waits, fn, inc in q:
                        for k, v in waits:
                            engine.wait_ge(self.semkey[k], v)
                        if fn is None:
                            continue
                        ins = fn(engine)
                        if inc is not None:
                            ins.then_inc(self.semkey[inc[0]], inc[1])

                getattr(block, engmap[e])(body)


class Arena:
    def __init__(self, ap, nbytes):
        self.ap = ap
        self.cap = nbytes
        self.off = 0

    def alloc(self, shape, dt, name=""):
        esz = 4 if dt == F32 else 2
        nfree = 1
        for s in shape[1:]:
            nfree *= s
        nb = nfree * esz
        o = self.off
        self.off += (nb + 31) // 32 * 32
        assert self.off <= self.cap, "SBUF arena overflow: %d > %d (%s)" % (self.off, self.cap, name)
        v = self.ap[0:shape[0], o // 2:o // 2 + nb // 2]
        if dt == F32:
            v = v.bitcast(F32)
        if len(shape) == 3:
            v = v.rearrange("p (a b) -> p a b", a=shape[1])
        elif len(shape) == 4:
            v = v.rearrange("p (a b c) -> p a b c", a=shape[1], b=shape[2])
        return T(v, name)


PC_G = 0
PC_MCW = 40
PC_MCB = 72
PC_MNG = 80
PC_FCW = (84, 260)
PC_FCB = (216, 392)
PC_PSC = 436
PC_RNG = 440
PC_N = 444


def _cols(vec, nchunk):
    return np.ascontiguousarray(np.asarray(vec, np.float32).reshape(nchunk, 128).T)


def make_pcol(inp):
    pc = np.zeros((128, PC_N), np.float32)
    gs = [inp["norm_mix_g"][0], inp["norm_ffn_g"][0], inp["norm_mix_g"][1], inp["norm_ffn_g"][1], inp["norm_out_g"]]
    for n, g in enumerate(gs):
        pc[:, PC_G + n * 8:PC_G + (n + 1) * 8] = _cols(g, 8)
    cw = inp["mlstm_conv_w"][0]
    for j in range(4):
        pc[:, PC_MCW + j:PC_MCW + 32:4] = _cols(cw[j], 8)
    pc[:, PC_MCB:PC_MCB + 8] = _cols(inp["mlstm_conv_b"][0], 8)
    pc[:, PC_MNG:PC_MNG + 4] = _cols(inp["mlstm_norm_g"][0], 4)
    for l in range(2):
        fw = inp["ffn_conv_w"][l]
        for j in range(3):
            pc[:, PC_FCW[l] + j:PC_FCW[l] + 132:3] = _cols(fw[j], 44)
        pc[:, PC_FCB[l]:PC_FCB[l] + 44] = _cols(inp["ffn_conv_b"][l], 44)
    pc[:, PC_PSC:PC_PSC + 4] = _cols(inp["pool_scale"][0], 4)
    pc[:, PC_RNG:PC_RNG + 4] = _cols(inp["ret_norm_g"][0], 4)
    return pc


def make_cst():
    c = np.zeros((128, 512), np.float32)
    c[:, 0:128] = np.eye(128, dtype=np.float32)
    c[:, 128:256] = 1.0 / 1024.0
    c[:, 256:384] = 1.0 / 128.0
    c[:, 384:512] = 1.0
    return c


class Ctx:
    pass


def pipeline(items, stage_a, stage_b, skew, defer=3, hooks=()):
    n = len(items)
    pending = []
    hk = sorted([(min(n - 1, int(n * f)), i, fn) for i, (f, fn) in enumerate(hooks)])
    for t in range(n + skew):
        if t < n:
            stage_a(items[t])
        while hk and hk[0][0] <= t:
            hk.pop(0)[2]()
        if t >= skew:
            ep = stage_b(items[t - skew])
            pending = [(c - 1, f) for c, f in pending]
            due = [x for x in pending if x[0] <= 0]
            pending = [x for x in pending if x[0] > 0]
            for c, f in due:
                nxt = f()
                if nxt is not None:
                    pending.append((defer, nxt))
            if ep is not None:
                pending.append((defer, ep))
    while pending:
        c, f = pending.pop(0)
        nxt = f()
        if nxt is not None:
            pending.append((0, nxt))


def rmsnorm_group(P, xg, nidx, y, w=GT):
    S = P.S
    psms = P.nextps()
    for kc in range(8):
        sq = P.sq[kc % 2]
        S.act(sq[:, 0:w], xg[:, kc, 0:w], AF.Square)
        S.matmul(psms[:, 0:w], P.cbf[:, 128:256], sq[:, 0:w], start=(kc == 0), stop=(kc == 7), inc=True)
    S.act(P.rstd[:, 0:w], psms[:, 0:w], AF.Ln, bias=P.epsc[:, 0:1])
    S.act(P.rstd[:, 0:w], P.rstd[:, 0:w], AF.Exp, scale=-0.5)
    for kc in range(8):
        S.stt("dve", y[:, kc, 0:w], xg[:, kc, 0:w], P.pcol[:, PC_G + nidx * 8 + kc:PC_G + nidx * 8 + kc + 1],
              P.rstd[:, 0:w], ALU.mult, ALU.mult)


def phase_proj_ab(P):
    S, A, D_ = P.S, P.A, P.dram
    A.off = P.abase
    Wk = [A.alloc([128, 3592], BF16, "wab%d" % kc) for kc in range(8)]
    xg = [A.alloc([128, 8, GT], F32, "xg%d" % i) for i in range(2)]
    y = [A.alloc([128, 8, GT], BF16, "y%d" % i) for i in range(2)]
    qst = [A.alloc([64, 8, GT], BF16, "qst%d" % i) for i in range(2)]
    kst = [A.alloc([64, 8, GT], BF16, "kst%d" % i) for i in range(2)]
    avst = [A.alloc([128, 4, 512], BF16, "avst%d" % i) for i in range(2)]
    bvst = [A.alloc([128, 4, 512], BF16, "bvst%d" % i) for i in range(2)]
    bqkst = [A.alloc([128, 8, GT], BF16, "bqkst%d" % i) for i in range(2)]
    bost = [A.alloc([128, 4, GT], BF16, "bost%d" % i) for i in range(2)]
    gst = [A.alloc([8, GT], F32, "gst%d" % i) for i in range(2)]
    pre = [A.alloc([128, 3 + GT], F32, "pre%d" % i) for i in range(3)]
    acc = [A.alloc([128, GT], F32, "acc%d" % i) for i in range(3)]
    tail = [A.alloc([128, 4], F32, "tail%d" % c) for c in range(8)]
    for c in range(8):
        S.memset("dve", tail[c].v, 0.0)
    xTv = D_["xT"].rearrange("(kc p) t -> p kc t", p=128)
    stt_ = {"npre": 0}

    def pre_(g):
        S.dma("act", xg[g % 2].v, xTv[:, :, g * GT:(g + 1) * GT])

    def norm_(g):
        rmsnorm_group(P, xg[g % 2], 0, y[g % 2])

    def fm_chunk(yb, col0, m=128):
        ps = P.nextps()
        for kc in range(8):
            S.matmul(ps[0:m, :], Wk[kc][:, col0:col0 + m], yb[:, kc, :], start=(kc == 0), stop=(kc == 7))
        return ps

    def body1(g):
        b = g % 2
        ts_ = slice(g * GT, (g + 1) * GT)
        yb = y[b]
        if g == 0 and ARRIVAL_ORDER:
            for kc in range(8):
                for i in range(8):
                    S.matmul(P.ps[i].v, Wk[kc][:, i * 128:(i + 1) * 128], yb[:, kc, :], start=(kc == 0), stop=(kc == 7),
                             inc=(kc == 7))
        for c in range(4):
            ps = P.ps[c] if (g == 0 and ARRIVAL_ORDER) else fm_chunk(yb, c * 128)
            S.evac(qst[b][:, 2 * c, :], ps[0:64, :], scale=0.125)
            S.evac(qst[b][:, 2 * c + 1, :], ps[64:128, :], scale=0.125)
        S.dma("sp", D_["qa"].rearrange("h p t -> p h t")[:, :, ts_], qst[b].v)
        for c in range(4):
            ps = P.ps[4 + c] if (g == 0 and ARRIVAL_ORDER) else fm_chunk(yb, 512 + c * 128)
            S.evac(kst[b][:, 2 * c, :], ps[0:64, :])
            S.evac(kst[b][:, 2 * c + 1, :], ps[64:128, :])
        S.dma("sp", D_["ka"].rearrange("h p t -> p h t")[:, :, ts_], kst[b].v)
        for (col0, st, dname) in ((1024, avst[b], "va"), (2560, bvst[b], "bv")):
            for sub in range(4):
                ps = P.nextps()
                for kc in range(8):
                    S.matmul(ps.v, yb[:, kc, sub * 128:(sub + 1) * 128], Wk[kc][:, col0:col0 + 512],
                             start=(kc == 0), stop=(kc == 7))
                S.evac(st[:, sub, :], ps.v)
            S.dma("sp", D_[dname].rearrange("(n p) f -> p n f", p=128)[:, 4 * g:4 * g + 4, :], st.v)

    def body2(g):
        b = g % 2
        ts_ = slice(g * GT, (g + 1) * GT)
        yb = y[b]
        prev = None
        for c in range(8):
            ps = fm_chunk(yb, 1536 + c * 128)
            pr = pre[stt_["npre"] % 3]
            ac = acc[stt_["npre"] % 3]
            stt_["npre"] += 1
            S.copy("act", pr[:, 3:3 + GT], ps.v)
            S.copy("pool", pr[:, 0:3], tail[c][:, 0:3])
            S.copy("pool", tail[c][:, 0:3], pr[:, GT:GT + 3])
            cw = PC_MCW + c * 4
            S.act(ac.v, ps.v, AF.Identity, bias=P.pcol[:, PC_MCB + c:PC_MCB + c + 1],
                  scale=P.pcol[:, cw + 3:cw + 4])
            for j in range(3):
                S.stt("dve", ac.v, pr[:, j:j + GT], P.pcol[:, cw + j:cw + j + 1], ac.v, ALU.mult, ALU.add)
            if prev is not None:
                S.act(bqkst[b][:, prev[0], :], prev[1].v, AF.Silu)
            prev = (c, ac)
        S.act(bqkst[b][:, prev[0], :], prev[1].v, AF.Silu)
        S.dma("sp", D_["bqk"].rearrange("c p t -> p c t")[:, :, ts_], bqkst[b].v)
        for c in range(4):
            ps = fm_chunk(yb, 3072 + c * 128)
            S.act(bost[b][:, c, :], ps.v, AF.Sigmoid)
        S.dma("sp", D_["bo"].rearrange("c p t -> p c t")[:, :, ts_], bost[b].v)
        ps = fm_chunk(yb, 3584, m=8)
        S.copy("dve", gst[b].v, ps[0:8, :])
        S.dma("sp", D_["gates"][:, ts_], gst[b].v)

    pre_(0)
    for kc in range(8):
        S.dma("pool", Wk[kc].v, D_["w_in_ab"][kc * 128:(kc + 1) * 128, :], after=xg[0].last_w)
    norm_(0)
    for g in range(NG):
        if g + 1 < NG:
            pre_(g + 1)
        body1(g)
        if g + 1 < NG:
            norm_(g + 1)
        body2(g)
    S.barrier()


def build_program(upto=99, debug=()):
    nc = bass.Bass("TRN2", target_bir_lowering=False)
    P = Ctx()
    P.nc = nc
    dram = {}

    def din(name, shape):
        dram[name] = nc.dram_tensor(name, list(shape), F32, kind="ExternalInput").ap()

    def dscr(name, shape, dt):
        kind = "ExternalOutput" if name in debug else "Internal"
        dram[name] = nc.dram_tensor(name, list(shape), dt, kind=kind).ap()

    din("xT", (D, S_LEN))
    din("cst", (128, 512))
    din("pcol", (128, PC_N))
    din("w_in_ab", (D, 3592))
    dscr("qa", (8, 64, S_LEN), BF16)
    dscr("ka", (8, 64, S_LEN), BF16)
    dscr("va", (S_LEN, 512), BF16)
    dscr("bv", (S_LEN, 512), BF16)
    dscr("bqk", (8, 128, S_LEN), BF16)
    dscr("bo", (4, 128, S_LEN), BF16)
    dscr("gates", (8, S_LEN), F32)
    din("tb", (128, 2048))
    din("b31", (128, 8))
    din("sqm", (128, 256))
    din("mobac", (128, 1024))
    din("blk1h", (16, S_LEN))
    dscr("catT", (D, S_LEN), BF16)
    din("gb", (4, 2))
    dscr("grow", (8, S_LEN), F32)
    dscr("erow", (4, 32), F32)
    if "dbgC" in debug:
        dscr("dbgC", (128, 32 * 130), BF16)
        dscr("dbgK", (128, 32 * 128), BF16)
        dscr("dbgV", (128, 32 * 130), BF16)
        dscr("dbgE", (128, 32), F32)
        dscr("dbgQ", (128, S_LEN), BF16)
    din("w_out_ab", (D, D))
    din("w_out_cd", (D, D))
    for l in range(2):
        din("ffn_w_up%d" % l, (D, 5632))
        din("ffn_w_down%d" % l, (2816, D))
    dscr("h1T", (D, S_LEN), F32)
    din("w_in_cd", (D, 2048))
    din("w_rot", (D, 512))
    din("rtab", (512, 2 * S_LEN))
    din("invc", (128, 64))
    din("pool_w", (512, 128))
    dscr("rqk", (8, 64, S_LEN), BF16)
    dscr("rv", (S_LEN, 512), BF16)
    dscr("rg", (4, 128, S_LEN), BF16)
    dscr("hT", (D, S_LEN), F32)
    dscr("aT", (2816, S_LEN), BF16)
    dram["outT"] = nc.dram_tensor("outT", [D, S_LEN], F32, kind="ExternalOutput").ap()
    P.dram = dram

    with contextlib.ExitStack() as st:
        S = Sched(nc, st)
        P.S = S
        NBYTES = 207 * 1024
        arena_ap = st.enter_context(nc.sbuf_tensor("arena", [128, NBYTES // 2], BF16)).ap()
        A = Arena(arena_ap, NBYTES)
        P.A = A
        P.ps = [T(st.enter_context(nc.psum_tensor("ps%d" % i, [128, 512], F32)).ap(), "ps%d" % i) for i in range(8)]
        P.psi = 0
        P.pref = {}
        P.pending_w = []

        def nextps():
            P.psi = (P.psi + 1) % 8
            return P.ps[P.psi]
        P.nextps = nextps
        P.psrc = {}

        def psr(lo, hi):
            i = P.psrc.get((lo, hi), lo)
            P.psrc[(lo, hi)] = lo + (i + 1 - lo) % (hi - lo)
            return P.ps[i]
        P.psr = psr
        P.cbf = A.alloc([128, 512], BF16, "cbf")
        S.dma("pool", P.cbf.v, dram["cst"])
        P.pcol = A.alloc([128, PC_N], F32, "pcol")
        S.dma("sp", P.pcol.v, dram["pcol"])
        P.c01 = A.alloc([128, 128], BF16, "c01")
        S.dma("pool", P.c01.v, dram["sqm"][:, 128:256])
        P.epsc = A.alloc([128, 2], F32, "epsc")
        S.memset("dve", P.epsc.v, EPS)
        P.sq = [A.alloc([128, GT], BF16, "sq%d" % i) for i in range(2)]
        P.rstd = A.alloc([128, GT], F32, "rstd")
        P.abase = A.off

        if upto >= 1:
            phase_proj_ab(P)
        if upto >= 2:
            phase_moba(P)
        if upto >= 3:
            phase_mlstm(P, pf=[("wo", dram["w_out_ab"], 8, 1024), ("wu", dram["ffn_w_up0"], 2, 5632)])
        if upto >= 4:
            phase_outproj_ffnup(P, 0, "xT", "w_out_ab", "ffn_w_up0")
        if upto >= 5:
            phase_ffndown(P, 0, "ffn_w_down0", final=False,
                          pf=[("wcd", dram["w_in_cd"], 8, 2048), ("wrot", dram["w_rot"], 8, 512)])
        if upto >= 6:
            phase_proj_cd(P)
        if upto >= 7:
            phase_retention(P, pf=[("wo", dram["w_out_cd"], 8, 1024), ("wu", dram["ffn_w_up1"], 7, 5632)])
        if upto >= 8:
            phase_outproj_ffnup(P, 1, "hT", "w_out_cd", "ffn_w_up1")
        if upto >= 9:
            phase_ffndown(P, 1, "ffn_w_down1", final=True)
        S.barrier()
        S.emit()
    return nc


def _rel_bucket_np(d):
    d = np.maximum(d, 0)
    df = np.maximum(d, 1).astype(np.float32)
    large = 16 + (np.log(df / np.float32(16)) / np.float32(math.log(128 / 16)) * np.float32(16)).astype(np.int32)
    large = np.minimum(large, 31)
    return np.where(d < 16, d, large)


def make_moba_consts(rel_bias):
    k = np.arange(128)[:, None]
    q = np.arange(128)[None, :]
    tb = np.zeros((128, 8, 2, 128), np.float32)
    for di, delta in enumerate((0, 128)):
        idx = _rel_bucket_np(q - k + delta)
        for h in range(8):
            tb[:, h, di, :] = rel_bias[idx, h]
    b31 = np.ascontiguousarray(np.broadcast_to(rel_bias[31][None, :], (128, 8))).astype(np.float32)
    caus = np.where(q >= k, 0.0, -BIG).astype(np.float32)
    c01 = (q >= k).astype(np.float32)
    i = np.arange(32)[:, None]
    n = np.arange(16)[None, :]
    cur = i // 2
    pastm = np.where(n < cur, 0.0, -1e30).astype(np.float32)
    cmask = np.where(n == cur, 0.0, -BIG).astype(np.float32)
    mobac = np.zeros((128, 2, 512), np.float32)
    mobac[:, 0, :] = pastm.reshape(1, 512)
    mobac[:, 1, :] = cmask.reshape(1, 512)
    blk1h = (np.arange(4096)[None, :] // 256 == np.arange(16)[:, None]).astype(np.float32)
    sqm = np.concatenate([caus, c01], axis=1)
    return {"tb": tb.reshape(128, 8 * 2 * 128), "b31": b31, "sqm": sqm, "mobac": mobac.reshape(128, 1024), "blk1h": blk1h}


def phase_moba(P):
    S, A, D_ = P.S, P.A, P.dram
    A.off = P.abase
    Qa = [A.alloc([80, S_LEN], BF16, "Qa%d" % i) for i in range(2)]
    Ka = [A.alloc([80, S_LEN], BF16, "Ka%d" % i) for i in range(2)]
    Vt = [A.alloc([128, 32, 65], BF16, "Vt%d" % i) for i in range(2)]
    tb = A.alloc([128, 8, 2, 128], F32, "tb")
    b31 = A.alloc([128, 8], F32, "b31")
    sqm = A.alloc([128, 256], F32, "sqm")
    mobac = A.alloc([128, 2, 512], F32, "mobac")
    Tb = [A.alloc([128, 2, 128], BF16, "Tb%d" % i) for i in range(2)]
    km32 = A.alloc([64, 16], F32, "km32")
    kmb = A.alloc([64, 16], BF16, "kmb")
    gm = A.alloc([128, 32, 16], F32, "gm")
    sel = A.alloc([128, 32, 16], F32, "sel")
    top8 = A.alloc([128, 32, 8], F32, "top8")
    thr = A.alloc([128, 32, 1], F32, "thr")
    M80 = A.alloc([128, 32, 80], BF16, "M80")
    PT = [A.alloc([128, GT], BF16, "PT%d" % i) for i in range(5)]
    rd = [A.alloc([65, GT], F32, "rd%d" % i) for i in range(2)]
    ones32 = A.alloc([65, 64], F32, "ones32")
    bcs = [A.alloc([64, GT], F32, "bcs%d" % i) for i in range(2)]
    ost = [A.alloc([64, GT], BF16, "ost%d" % i) for i in range(2)]
    S.dma("sp", tb.v, D_["tb"].rearrange("p (h d q) -> p h d q", h=8, d=2))
    S.dma("sp", b31.v, D_["b31"])
    S.dma("sp", sqm.v, D_["sqm"])
    S.dma("sp", mobac.v, D_["mobac"].rearrange("p (a b) -> p a b", a=2))
    S.memset("dve", M80.v, 0.0)
    S.memset("dve", ones32.v, 1.0)
    for i in range(2):
        S.dma("pool", Ka[i][64:80, :], D_["blk1h"])
        S.memset("dve", Vt[i][:, :, 64:65], 1.0)
    ident = P.cbf[:, 0:128]
    vav = D_["va"].rearrange("(n p) f -> p n f", p=128)
    st = {"npt": 0, "po": None}

    def mlstm_gate_prep():
        I4 = A.alloc([4, S_LEN], F32, "I4")
        F4 = A.alloc([4, S_LEN], F32, "F4")
        t0 = A.alloc([4, S_LEN], F32, "gt0")
        t1 = A.alloc([4, S_LEN], F32, "gt1")
        t2 = A.alloc([4, S_LEN], F32, "gt2")
        ones4 = A.alloc([4, S_LEN], F32, "ones4")
        gb4 = A.alloc([4, 2], F32, "gb4")
        onec = A.alloc([4, 2], F32, "onec4")
        S.dma("sp", gb4.v, D_["gb"])
        S.dma("sp", I4.v, D_["gates"][0:4, :])
        S.dma("sp", F4.v, D_["gates"][4:8, :])
        S.memset("dve", ones4.v, 1.0)
        S.memset("dve", onec.v, 1.0)
        lnscale = math.log(128.0 ** -0.5)
        S.act(t0.v, F4.v, AF.Abs, bias=gb4[:, 1:2])
        S.act(t0.v, t0.v, AF.Exp, scale=-1.0)
        S.act(t0.v, t0.v, AF.Ln, bias=onec[:, 0:1])
        S.ts("dve", t1.v, F4.v, gb4[:, 1:2], 0.0, ALU.add, ALU.min)
        S.tt("dve", t1.v, t1.v, t0.v, ALU.subtract)
        S.op("dve", lambda e, o=t2.ap, d0=ones4.ap, d1=t1.ap: e.tensor_tensor_scan(o, d0, d1, 0.0, ALU.mult, ALU.add),
             reads=[ones4, t1], writes=[t2])
        S.ts("dve", t0.v, I4.v, gb4[:, 0:1], lnscale, ALU.add, ALU.add)
        S.tt("dve", t0.v, t0.v, t2.v, ALU.subtract)
        t3 = A.alloc([4, S_LEN], F32, "gt3")
        er = A.alloc([4, 32], F32, "ger")
        t2v = V(t2, t2.ap.rearrange("p (c i) -> p c i", i=128))
        t3v = V(t3, t3.ap.rearrange("p (c i) -> p c i", i=128))
        S.memset("dve", t3[:, 0:128], 0.0)
        S.copy("dve", t3v[:, 1:32, :], V(t2, t2v.ap[:, 0:31, 127:128].broadcast_to([4, 31, 128])))
        S.tt("dve", t1.v, t2.v, t3.v, ALU.subtract)
        S.act(t1.v, t1.v, AF.Exp)
        S.tt("dve", t0.v, t0.v, t3.v, ALU.add)
        S.act(t0.v, t0.v, AF.Exp)
        S.tt("dve", er[:, 1:32], V(t2, t2v.ap[:, 1:32, 127]), V(t2, t2v.ap[:, 0:31, 127]), ALU.subtract)
        S.copy("dve", er[:, 0:1], t2[:, 127:128])
        S.act(er.v, er.v, AF.Exp)
        S.dma("sp", D_["grow"][0:4, :], t1.v)
        S.dma("sp", D_["grow"][4:8, :], t0.v)
        S.dma("sp", D_["erow"], er.v)

    def loads(h):
        b = h % 2
        S.dma("sp", Qa[b][0:64, :], D_["qa"][h])
        S.dma("sp", Ka[b][0:64, :], D_["ka"][h])
        S.dma("sp", Vt[b][:, :, 0:64], vav[:, :, h * 64:(h + 1) * 64])

    def preamble_pieces(h):
        b = h % 2
        stp = {}

        def p1():
            S.stt("dve", Tb[b][:, 0, :], tb[:, h, 0, :], b31[:, h:h + 1], sqm[:, 0:128], ALU.subtract, ALU.add)
            S.ts("dve", Tb[b][:, 1, :], tb[:, h, 1, :], b31[:, h:h + 1], None, ALU.subtract)
            kv = V(Ka[b], Ka[b].ap[0:64, :].rearrange("p (n s) -> p n s", s=256))
            S.op("dve", lambda e, o=km32.ap, i_=kv.ap: e.tensor_reduce(o, i_, AX.X, ALU.add), reads=[kv], writes=[km32])
            S.ts("dve", kmb.v, km32.v, 1.0 / 256.0, None, ALU.mult)

        def p2():
            psg = P.psr(6, 8)
            for i in range(32):
                S.matmul(psg[:, i * 16:(i + 1) * 16], Qa[b][0:64, i * 128:(i + 1) * 128], kmb.v)
            psg3 = V(psg, psg.ap.rearrange("p (a b) -> p a b", a=32))
            S.tt("dve", gm.v, psg3, V(mobac, mobac.ap[:, 0, :].rearrange("p (a b) -> p a b", a=32)), ALU.add)

        def p3():
            for i in range(16):
                S.op("dve", lambda e, o=top8.ap[:, i, :], i_=gm.ap[:, i, :]: e.max(o, i_), reads=[gm], writes=[top8])

        def p4():
            for i in range(16, 32):
                S.op("dve", lambda e, o=top8.ap[:, i, :], i_=gm.ap[:, i, :]: e.max(o, i_), reads=[gm], writes=[top8])
            S.ts("dve", thr.v, top8[:, :, 2:3], -1e29, None, ALU.max)
            S.tt("dve", sel.v, gm.v, V(thr, thr.ap.broadcast_to([128, 32, 16])), ALU.is_ge)
            S.stt("dve", M80[:, :, 64:80], sel.v, BIG, V(mobac, mobac.ap[:, 1, :].rearrange("p (a b) -> p a b", a=32)),
                  ALU.mult, ALU.add)

        def p5(G):
            pst = P.psr(6, 8)
            pstb = V(pst, pst.ap.bitcast(BF16))
            for r in range(4):
                S.transpose(pstb[0:80, r * 128:(r + 1) * 128], M80[:, 4 * G + r, :], ident)
            S.copy("dve", Qa[b][64:80, G * GT:(G + 1) * GT], pstb[64:80, 0:GT])

        return [p1, p2, p3, p4] + [(lambda G=G: p5(G)) for G in range(NG)]

    def preamble(h):
        for f in preamble_pieces(h):
            f()

    def mainloop(h, hooks):
        b = h % 2
        items = []
        for G in range(NG):
            nj = 4 * G + 4
            for j in range(nj):
                items.append({"G": G, "j": j, "first": j == 0, "last": j == nj - 1})

        def stage_a(it):
            G, j = it["G"], it["j"]
            r0 = max(0, j - 4 * G)
            c0 = r0 * 128
            pss = P.psr(2, 6)
            adds = []
            for r in range(r0, 4):
                d = 4 * G + r - j
                if d in (0, 1):
                    adds.append((r, d))
            S.matmul(pss[:, c0:GT], Ka[b][0:80, j * 128:(j + 1) * 128], Qa[b][0:80, G * GT + c0:(G + 1) * GT],
                     start=True, stop=(len(adds) == 0), inc=True)
            for ai, (r, d) in enumerate(adds):
                S.matmul(pss[:, r * 128:(r + 1) * 128], ident, Tb[b][:, d, :], start=False,
                         stop=(ai == len(adds) - 1), inc=True)
            it["pss"], it["c0"] = pss, c0

        def stage_b(it):
            G, j, pss, c0 = it["G"], it["j"], it["pss"], it["c0"]
            if it["first"]:
                st["po"] = P.psr(0, 2)
            po = st["po"]
            pt = PT[st["npt"] % 5]
            st["npt"] += 1
            S.act(pt[:, c0:GT], pss[:, c0:GT], AF.Exp)
            S.matmul(po[0:65, c0:GT], Vt[b][:, j, :], pt[:, c0:GT], start=it["first"], stop=it["last"], inc=True)
            if not it["last"]:
                return None

            def epilogue(po=po, G=G):
                k = st["nep"] = st.get("nep", 0) + 1
                rd_, bcs_ = rd[k % 2], bcs[k % 2]
                S.act(rd_[64:65, :], po[64:65, :], AF.Ln)
                S.act(rd_[64:65, :], rd_[64:65, :], AF.Exp, scale=-1.0)
                pbc = P.psr(6, 8)
                S.matmul(pbc[0:64, :], ones32[64:65, :], rd_[64:65, :])
                S.copy("dve", bcs_.v, pbc[0:64, :])
                o_ = ost[k % 2]
                S.tt("dve", o_.v, po[0:64, :], bcs_.v, ALU.mult)
                S.dma("sp", D_["catT"][h * 64:(h + 1) * 64, G * GT:(G + 1) * GT], o_.v)
            return epilogue

        pipeline(items, stage_a, stage_b, 3, defer=1, hooks=hooks)

    loads(0)
    preamble(0)
    for h in range(8):
        hooks = []
        if h + 1 < 8:
            loads(h + 1)
            pcs = preamble_pieces(h + 1)
            fr = [0.2, 0.3, 0.4, 0.5] + [0.58 + 0.045 * i for i in range(NG)]
            hooks += list(zip(fr, pcs))
        if h == 2:
            hooks.append((0.1, mlstm_gate_prep))
        mainloop(h, hooks)
    S.barrier()


def make_inputs(inp, b):
    im = {"xT": np.ascontiguousarray(inp["x"][b].T), "cst": make_cst(), "pcol": make_pcol(inp),
          "w_in_ab": np.ascontiguousarray(inp["w_in_ab"][0])}
    im.update(make_moba_consts(np.asarray(inp["rel_bias"], np.float32)))
    im["w_out_ab"] = np.ascontiguousarray(inp["w_out_ab"][0])
    im["w_out_cd"] = np.ascontiguousarray(inp["w_out_cd"][0])
    for l in range(2):
        im["ffn_w_up%d" % l] = np.ascontiguousarray(inp["ffn_w_up"][l])
        im["ffn_w_down%d" % l] = np.ascontiguousarray(inp["ffn_w_down"][l])
    im.update(make_cd_consts(inp))
    im["gb"] = np.ascontiguousarray(np.stack([inp["mlstm_b_i"][0], inp["mlstm_b_f"][0]], axis=1).astype(np.float32))
    return im


def phase_mlstm(P, pf=()):
    S, A, D_ = P.S, P.A, P.dram
    pf_issue = prefetch_w(P, pf)
    NCH = 32
    bq = [A.alloc([128, S_LEN], BF16, "bq%d" % i) for i in range(2)]
    bk = [A.alloc([128, S_LEN], BF16, "bk%d" % i) for i in range(2)]
    og = [A.alloc([128, S_LEN], BF16, "og%d" % i) for i in range(2)]
    Vt = [A.alloc([128, NCH, 130], BF16, "mVt%d" % i) for i in range(2)]
    qsc = A.alloc([128, S_LEN], F32, "qsc")
    ksc = A.alloc([128, S_LEN], F32, "ksc")
    kdT = [A.alloc([128, NCH, 128], BF16, "kdT%d" % i) for i in range(2)]
    Cbf = [A.alloc([128, NCH, 130], BF16, "Cbf%d" % i) for i in range(2)]
    T32 = [[A.alloc([128, 130], F32, "T32%d_%d" % (i, k)) for k in range(2)] for i in range(2)]
    ecol = [A.alloc([128, NCH], F32, "ecol%d" % i) for i in range(2)]
    c01 = P.c01
    St = [A.alloc([128, 4, 128], BF16, "St%d" % i) for i in range(3)]
    drow = [A.alloc([1, GT], F32, "drow%d" % i) for i in range(2)]
    e2row = [A.alloc([1, GT], BF16, "e2row%d" % i) for i in range(2)]
    sqb = [A.alloc([128, GT], BF16, "sqb%d" % i) for i in range(2)]
    rs = [A.alloc([128, GT], F32, "rs%d" % i) for i in range(2)]
    y1 = [A.alloc([128, GT], F32, "y1%d" % i) for i in range(2)]
    yst = [A.alloc([128, GT], BF16, "yst%d" % i) for i in range(2)]
    for i in range(2):
        S.memset("dve", Vt[i][:, :, 128:129], 1.0)
        S.memset("dve", Vt[i][:, :, 129:130], 0.0)
    bvv = D_["bv"].rearrange("(n p) f -> p n f", p=128)
    ident = P.cbf[:, 0:128]
    st = {"nst": 0, "nep": 0, "ntmp": 0}
    psd_slots = [P.ps[6].sub((slice(None), slice(k * 130, (k + 1) * 130)), "psd%d" % k) for k in range(3)]
    cm1 = A.alloc([1, 2], F32, "cm1")
    S.memset("dve", cm1[:, 0:1], -1.0)
    S.memset("dve", cm1[:, 1:2], EPS)

    def loads(h):
        b = h % 2
        S.dma("sp", qsc.v, D_["grow"][h:h + 1, :].partition_broadcast(128))
        S.dma("sp", ksc.v, D_["grow"][4 + h:5 + h, :].partition_broadcast(128))
        S.dma("sp", ecol[b].v, D_["erow"][h:h + 1, :].partition_broadcast(128))
        S.dma("sp", bq[b].v, D_["bqk"][h])
        S.dma("sp", bk[b].v, D_["bqk"][4 + h])
        S.dma("sp", Vt[b][:, :, 0:128], bvv[:, :, h * 128:(h + 1) * 128])
        S.dma("sp", og[b].v, D_["bo"][h])

    def pre_scale(h):
        b = h % 2
        for G in range(NG):
            gs = slice(G * GT, (G + 1) * GT)
            S.tt("dve", bq[b][:, gs], bq[b][:, gs], qsc[:, gs], ALU.mult)
            S.tt("dve", bk[b][:, gs], bk[b][:, gs], ksc[:, gs], ALU.mult)
        for q in range(4):
            pst = P.psr(4, 6)
            pstb = V(pst, pst.ap.bitcast(BF16))
            for r in range(8):
                c = q * 8 + r
                S.transpose(pstb[:, r * 128:(r + 1) * 128], bk[b][:, c * 128:(c + 1) * 128], ident)
            S.evac(kdT[b][:, q * 8:(q + 1) * 8, :], V(pst, pst.ap.bitcast(BF16).rearrange("p (a b) -> p a b", a=8)))

    def pre_scan(h, q):
        b = h % 2
        for r in range(4):
            c = q * 4 + r
            if c == NCH - 1:
                continue
            psd = psd_slots[c % 3]
            S.matmul(psd.v, kdT[b][:, c, :], Vt[b][:, c, :])
            tn, tp = T32[b][c % 2], T32[b][(c + 1) % 2]
            if c == 0:
                S.ts("dve", tn.v, psd.v, 1.0, None, ALU.mult)
            else:
                S.stt("dve", tn.v, tp.v, ecol[b][:, c - 1:c], psd.v, ALU.mult, ALU.add)
            S.ts("dve", Cbf[b][:, c + 1, :], tn.v, ecol[b][:, c:c + 1], None, ALU.mult)

    def main(h, hook):
        b = h % 2

        def stage_a(it):
            G = it["G"]
            pss = P.psr(4, 6)
            for r in range(4):
                c = 4 * G + r
                cs = slice(c * 128, (c + 1) * 128)
                S.matmul(pss[:, r * 128:(r + 1) * 128], bk[b][:, cs], bq[b][:, cs])
            st_ = St[st["nst"] % 3]
            st["nst"] += 1
            S.tt("dve", st_.v, V(pss, pss.ap.rearrange("p (a b) -> p a b", a=4)),
                 V(c01, c01.ap.unsqueeze(1).broadcast_to([128, 4, 128])), ALU.mult)
            it["st"] = st_
            if hook is not None:
                hook(G)

        def stage_b(it):
            G, st_ = it["G"], it["st"]
            po = P.psr(0, 2)
            pden = P.psr(2, 4)
            for r in range(4):
                c = 4 * G + r
                cs = slice(c * 128, (c + 1) * 128)
                rs_ = slice(r * 128, (r + 1) * 128)
                S.matmul(po[:, rs_], Vt[b][:, c, 0:128], st_[:, r, :], start=True, stop=(c == 0), inc=True)
                if c > 0:
                    S.matmul(po[:, rs_], Cbf[b][:, c, 0:128], bq[b][:, cs], start=False, stop=True, inc=True)
                S.matmul(pden[0:1, rs_], P.cbf[:, 384:385], st_[:, r, :], start=True, stop=(c == 0), inc=True)
                if c > 0:
                    S.matmul(pden[0:1, rs_], Cbf[b][:, c, 128:129], bq[b][:, cs], start=False, stop=True, inc=True)

            def epilogue(po=po, pden=pden, G=G):
                k = st["nep"] = st["nep"] + 1
                k %= 2
                S.act(drow[k].v, pden[0:1, :], AF.Square)
                S.act(drow[k].v, drow[k].v, AF.Relu, bias=cm1[:, 0:1])
                S.act(e2row[k].v, drow[k].v, AF.Identity, bias=cm1[:, 1:2], scale=EPS)
                S.act(sqb[k].v, po.v, AF.Square)
                pn = P.psr(7, 8)
                S.matmul(pn.v, P.cbf[:, 256:384], sqb[k].v, start=True, stop=False, inc=True)
                S.matmul(pn.v, P.cbf[0:1, 384:512], e2row[k].v, start=False, stop=True, inc=True)
                S.act(rs[k].v, pn.v, AF.Ln)
                S.act(rs[k].v, rs[k].v, AF.Exp, scale=-0.5)
                S.copy("act", y1[k].v, po.v)

                def epilogue_b():
                    S.tt("dve", y1[k].v, y1[k].v, rs[k].v, ALU.mult)
                    S.stt("dve", yst[k].v, y1[k].v, P.pcol[:, PC_MNG + h:PC_MNG + h + 1], og[b][:, G * GT:(G + 1) * GT],
                          ALU.mult, ALU.mult)
                    S.dma("sp", D_["catT"][512 + h * 128:512 + (h + 1) * 128, G * GT:(G + 1) * GT], yst[k].v)
                return epilogue_b
            return epilogue

        pipeline([{"G": G} for G in range(NG)], stage_a, stage_b, 1, defer=1)

    loads(0)
    pf_issue(after=og[0].last_w + Vt[0].last_w + bk[0].last_w)
    pre_scale(0)
    for q in range(8):
        pre_scan(0, q)
    if "dbgC" in D_:
        S.dma("sp", D_["dbgC"], Cbf[0].ap.rearrange("p a b -> p (a b)") if False else V(Cbf[0], Cbf[0].ap.rearrange("p a b -> p (a b)")))
        S.dma("sp", D_["dbgK"], V(kdT[0], kdT[0].ap.rearrange("p a b -> p (a b)")))
        S.dma("sp", D_["dbgV"], V(Vt[0], Vt[0].ap.rearrange("p a b -> p (a b)")))
        S.dma("sp", D_["dbgE"], ecol[0].v)
        S.dma("sp", D_["dbgQ"], bq[0].v)
    sched = {2: [], 3: [0, 1], 4: [2, 3], 5: [4, 5], 6: [6], 7: [7]}
    for h in range(4):
        hook = None
        if h + 1 < 4:
            loads(h + 1)

            def hook(G, h=h):
                if G == 2:
                    pre_scale(h + 1)
                for q in sched.get(G, []):
                    pre_scan(h + 1, q)
        main(h, hook)
    S.barrier()


def load_w_bf16(P, name, dram_ap, nk, ncols, issue=True):
    tiles = []
    for k in range(nk):
        off = P.A.off
        t = P.A.alloc([128, ncols], BF16, "%s%d" % (name, k))
        pf = P.pref.get((name, k))
        if pf is not None and pf[0] == off:
            tiles.append(pf[1])
            continue
        P.pending_w.append((t, dram_ap[k * 128:(k + 1) * 128, :]))
        tiles.append(t)
    if issue:
        flush_w(P)
    return tiles


def flush_w(P, after=()):
    for t, src in P.pending_w:
        P.S.dma("pool", t.v, src, after=after)
    P.pending_w = []


def prefetch_w(P, specs):
    P.pref = {}
    P.A.off = P.abase
    todo = []
    for name, dram_ap, nk, ncols in specs:
        for k in range(nk):
            off = P.A.off
            t = P.A.alloc([128, ncols], BF16, "%s%d" % (name, k))
            todo.append((t, dram_ap[k * 128:(k + 1) * 128, :]))
            P.pref[(name, k)] = (off, t)

    def issue(after=()):
        for t, src in todo:
            P.S.dma("pool", t.v, src, after=after)
    return issue


def phase_outproj_ffnup(P, layer, h_in, wout_name, wup_name):
    S, A, D_ = P.S, P.A, P.dram
    A.off = P.abase
    Wo = load_w_bf16(P, "wo", D_[wout_name], 8, 1024, issue=False)
    Wu = load_w_bf16(P, "wu", D_[wup_name], 8, 5632, issue=False)
    catg = A.alloc([128, 8, GT], BF16, "catg")
    xg = [A.alloc([128, 8, GT], F32, "xg%d" % i) for i in range(2)]
    y = [A.alloc([128, 8, GT], BF16, "y%d" % i) for i in range(2)]
    ast = [A.alloc([128, 11, GT], BF16, "ast%d" % i) for i in range(2)]
    ga = [A.alloc([128, GT], F32, "ga%d" % i) for i in range(3)]
    va = [A.alloc([128, GT], F32, "va%d" % i) for i in range(3)]
    hv = D_[h_in].rearrange("(kc p) t -> p kc t", p=128)
    h1v = D_["h1T"].rearrange("(kc p) t -> p kc t", p=128)
    cv = D_["catT"].rearrange("(kc p) t -> p kc t", p=128)
    av = D_["aT"].rearrange("(c p) t -> p c t", p=128)
    cwb, cbb = PC_FCW[layer], PC_FCB[layer]
    stt_ = {"n": 0}
    STEP = GT - 2
    ngr = (S_LEN + STEP - 1) // STEP
    rng = []
    for g in range(ngr):
        s0 = g * STEP
        e0 = min(S_LEN, s0 + STEP)
        rng.append((s0, e0, e0 - s0 + 2))

    def pre_(g):
        s0, e0, w = rng[g]
        if g == 0:
            S.memset("dve", catg[:, :, 0:2], 0.0)
            S.memset("dve", xg[0][:, :, 0:2], 0.0)
            S.dma("act", catg[:, :, 2:w], cv[:, :, s0:e0])
            S.dma("act", xg[0][:, :, 2:w], hv[:, :, s0:e0])
        else:
            S.dma("act", catg[:, :, 0:w], cv[:, :, s0 - 2:e0])
            S.dma("act", xg[g % 2][:, :, 0:w], hv[:, :, s0 - 2:e0])

    def norm_(g):
        b = g % 2
        s0, e0, w = rng[g]
        for dc in range(8):
            ps = P.nextps()
            for kc in range(8):
                S.matmul(ps[:, 0:w], Wo[kc][:, dc * 128:(dc + 1) * 128], catg[:, kc, 0:w], start=(kc == 0), stop=(kc == 7))
            S.tt("dve", xg[b][:, dc, 0:w], ps[:, 0:w], xg[b][:, dc, 0:w], ALU.add)
        S.dma("sp", h1v[:, :, s0:e0], xg[b][:, :, 2:w])
        rmsnorm_group(P, xg[b], 1 + 2 * layer, y[b], w=w)

    def body(g, fcs, early=False):
        s0, e0, w = rng[g]
        wn = w - 2
        yb = y[g % 2]
        if early:
            chs = [half * 22 + fc for fc in fcs for half in (0, 1)]
            assert len(chs) == 8
            for kc in range(8):
                for i, ch in enumerate(chs):
                    S.matmul(P.ps[i][:, 0:w], Wu[kc][:, ch * 128:(ch + 1) * 128], yb[:, kc, 0:w], start=(kc == 0),
                             stop=(kc == 7), inc=(kc == 7))
        for fi, fc in enumerate(fcs):
            n = stt_["n"]
            stt_["n"] += 1
            accs = []
            for half, abuf in ((0, ga), (1, va)):
                ch = half * 22 + fc
                if early:
                    ps = P.ps[fi * 2 + half]
                else:
                    ps = P.nextps()
                for kc in range(0 if early else 8):
                    S.matmul(ps[:, 0:w], Wu[kc][:, ch * 128:(ch + 1) * 128], yb[:, kc, 0:w], start=(kc == 0), stop=(kc == 7))
                ac = abuf[n % 3]
                w0 = P.pcol[:, cwb + ch * 3 + 0:cwb + ch * 3 + 1]
                w1 = P.pcol[:, cwb + ch * 3 + 1:cwb + ch * 3 + 2]
                w2 = P.pcol[:, cwb + ch * 3 + 2:cwb + ch * 3 + 3]
                bia = P.pcol[:, cbb + ch:cbb + ch + 1]
                S.act(ac[:, 0:wn], ps[:, 2:w], AF.Identity, bias=bia, scale=w2)
                S.stt("dve", ac[:, 0:wn], ps[:, 1:w - 1], w1, ac[:, 0:wn], ALU.mult, ALU.add)
                S.stt("dve", ac[:, 0:wn], ps[:, 0:wn], w0, ac[:, 0:wn], ALU.mult, ALU.add)
                accs.append(ac)
            S.act(accs[0][:, 0:wn], accs[0][:, 0:wn], AF.Silu)
            a_ = ast[(fc // 11 + 2 * g) % 2]
            S.tt("pool", a_[:, fc % 11, 0:wn], accs[0][:, 0:wn], accs[1][:, 0:wn], ALU.mult)
            if fc % 11 == 10:
                S.dma("sp", av[:, (fc // 11) * 11:(fc // 11) * 11 + 11, s0:e0], a_[:, :, 0:wn])

    pre_(0)
    flush_w(P, after=catg.last_w + xg[0].last_w)
    norm_(0)
    for g in range(ngr):
        if g + 1 < ngr:
            pre_(g + 1)
        if g == 0:
            norm_(1)
            body(0, range(0, 4), early=ARRIVAL_ORDER)
            body(0, range(4, 22))
            continue
        body(g, range(0, 8))
        if g + 1 < ngr and g + 1 != 1:
            norm_(g + 1)
        body(g, range(8, 22))
    S.barrier()


def phase_ffndown(P, layer, wdown_name, final, pf=()):
    S, A, D_ = P.S, P.A, P.dram
    pf_issue = prefetch_w(P, pf)
    Wd = load_w_bf16(P, "wd", D_[wdown_name], 22, 1024, issue=False)
    ag = [A.alloc([128, 22, GT], BF16, "ag%d" % i) for i in range(2)]
    hg = [A.alloc([128, 8, GT], F32, "hg%d" % i) for i in range(2)]
    yo = A.alloc([128, 8, GT], F32, "yo") if final else None
    h1v = D_["h1T"].rearrange("(kc p) t -> p kc t", p=128)
    ov = D_["outT" if final else "hT"].rearrange("(kc p) t -> p kc t", p=128)
    av = D_["aT"].rearrange("(c p) t -> p c t", p=128)
    def pre_(g):
        ts_ = slice(g * GT, (g + 1) * GT)
        S.dma("act", ag[g % 2].v, av[:, :, ts_])
        S.dma("sp", hg[g % 2].v, h1v[:, :, ts_])

    pre_(0)
    flush_w(P, after=ag[0].last_w + hg[0].last_w)
    pf_issue()
    for g in range(NG):
        b = g % 2
        ts_ = slice(g * GT, (g + 1) * GT)
        if g + 1 < NG:
            pre_(g + 1)
        if g == 0 and ARRIVAL_ORDER:
            for fc in range(22):
                for dc in range(8):
                    S.matmul(P.ps[dc].v, Wd[fc][:, dc * 128:(dc + 1) * 128], ag[b][:, fc, :], start=(fc == 0),
                             stop=(fc == 21), inc=(fc == 21))
            for dc in range(8):
                S.tt("dve", hg[b][:, dc, :], P.ps[dc].v, hg[b][:, dc, :], ALU.add)
        else:
            for dc in range(8):
                ps = P.nextps()
                for fc in range(22):
                    S.matmul(ps.v, Wd[fc][:, dc * 128:(dc + 1) * 128], ag[b][:, fc, :], start=(fc == 0), stop=(fc == 21))
                S.tt("dve", hg[b][:, dc, :], ps.v, hg[b][:, dc, :], ALU.add)
        if final:
            rmsnorm_group(P, hg[b], 4, yo)
            S.dma("sp", ov[:, :, ts_], yo.v)
        else:
            S.dma("sp", ov[:, :, ts_], hg[b].v)
    S.barrier()


RET_GAMMA = [1.0 - 2.0 ** (-5.0 - h) for h in range(4)]


def make_cd_consts(inp):
    w = np.asarray(inp["w_in_cd"][0], np.float32)
    qk = w[:, 512:1024].reshape(1024, 8, 2, 32)
    w_rot = np.ascontiguousarray(qk[:, :, ::-1, :].reshape(1024, 512))
    half = 32
    inv = (np.float32(10000.0) ** (-np.arange(half, dtype=np.float32) / np.float32(half))).astype(np.float32)
    ang = (np.arange(S_LEN, dtype=np.float32)[:, None] * inv[None, :]).astype(np.float32).astype(np.float64)
    cos, sin = np.cos(ang).T, np.sin(ang).T
    t = np.arange(S_LEN, dtype=np.float64)
    rtab = np.zeros((4, 128, 2, S_LEN), np.float64)
    for c2 in range(4):
        for p in range(128):
            hd = (c2 % 2) * 2 + p // 64
            pp = p % 64
            f = pp % 32
            lg = math.log(RET_GAMMA[hd])
            if c2 < 2:
                dfac = np.exp(lg * ((t % 128) + 1))
            else:
                dfac = np.exp(-lg * ((t % 128) + 1)) * (64.0 ** -0.5)
            rtab[c2, p, 0] = cos[f] * dfac
            rtab[c2, p, 1] = (-sin[f] if pp < 32 else sin[f]) * dfac
    invc = np.zeros((128, 4, 16), np.float32)
    for c, wdw in enumerate((2, 4, 8, 16)):
        invc[:, c, :] = (1.0 / np.minimum(np.arange(16) + 1, wdw)).astype(np.float32)[None, :]
    return {"w_in_cd": np.ascontiguousarray(w), "w_rot": w_rot, "rtab": rtab.astype(np.float32).reshape(4 * 128, 2 * S_LEN),
            "invc": invc.reshape(128, 64), "pool_w": np.ascontiguousarray(np.asarray(inp["pool_w"][0], np.float32).reshape(512, 128))}


def phase_proj_cd(P):
    S, A, D_ = P.S, P.A, P.dram
    A.off = P.abase
    Wk = load_w_bf16(P, "wcd", D_["w_in_cd"], 8, 2048)
    Wr = load_w_bf16(P, "wrot", D_["w_rot"], 8, 512)
    pw = A.alloc([128, 4, 128], BF16, "pw")
    S.dma("pool", pw.v, D_["pool_w"].rearrange("(g c) d -> c g d", g=4))
    invc = A.alloc([128, 4, 16], F32, "invc")
    S.dma("sp", invc.v, D_["invc"].rearrange("p (a b) -> p a b", a=4))
    xg = [A.alloc([128, 8, GT], F32, "xg%d" % i) for i in range(2)]
    y = [A.alloc([128, 8, GT], BF16, "y%d" % i) for i in range(2)]
    tab = [A.alloc([128, 2, GT], F32, "tab%d" % i) for i in range(2)]
    pre2 = [A.alloc([128, 16 + GT], F32, "pre%d" % i) for i in range(2)]
    sa2 = [A.alloc([128, 16 + GT], F32, "sa%d" % i) for i in range(2)]
    sb2 = [A.alloc([128, 16 + GT], F32, "sb%d" % i) for i in range(2)]
    tmp16 = A.alloc([128, 16], F32, "tmp16")
    pooled = [A.alloc([128, GT], BF16, "pooled%d" % i) for i in range(8)]
    cst_ = [A.alloc([128, GT], BF16, "cst%d" % i) for i in range(2)]
    t1 = A.alloc([128, GT], F32, "t1")
    t2 = A.alloc([128, GT], F32, "t2")
    rqst = [A.alloc([128, GT], BF16, "rqst%d" % i) for i in range(2)]
    rvst = [A.alloc([128, 4, 512], BF16, "rvst%d" % i) for i in range(2)]
    rgst = [A.alloc([128, 4, GT], BF16, "rgst%d" % i) for i in range(2)]
    tail = [A.alloc([128, 16], F32, "ptail%d" % c) for c in range(4)]
    for c in range(4):
        S.memset("dve", tail[c].v, 0.0)
    hv = D_["hT"].rearrange("(kc p) t -> p kc t", p=128)
    rqkv = D_["rqk"].rearrange("(c two) p t -> c (two p) t", two=2)
    rtv = D_["rtab"].rearrange("(c p) (a t) -> c p a t", p=128, a=2)
    stt_ = {"n": 0}
    mixq = []

    def pre_(g):
        S.dma("act", xg[g % 2].v, hv[:, :, g * GT:(g + 1) * GT])

    def norm_(g):
        rmsnorm_group(P, xg[g % 2], 2, y[g % 2])

    def fm_chunk(yb, W, col0):
        ps = P.nextps()
        for kc in range(8):
            S.matmul(ps.v, W[kc][:, col0:col0 + 128], yb[:, kc, :], start=(kc == 0), stop=(kc == 7))
        return ps

    def body1(g):
        b = g % 2
        ts_ = slice(g * GT, (g + 1) * GT)
        yb = y[b]
        for c in range(4):
            wdw = 2 ** (c + 1)
            ps = fm_chunk(yb, Wk, c * 128)
            pre, sa, sb = pre2[c % 2], sa2[c % 2], sb2[c % 2]
            S.copy("act", pre[:, 16:16 + GT], ps.v)
            S.copy("pool", pre[:, 0:16], tail[c].v)
            S.copy("pool", tail[c].v, pre[:, GT:GT + 16])
            src = pre
            for k in range(c + 1):
                sh = 2 ** k
                dst = sa if src is not sa else sb
                S.tt("dve" if k % 2 == 0 else "pool", dst[:, sh:16 + GT], src[:, sh:16 + GT], src[:, 0:16 + GT - sh], ALU.add)
                src = dst
            n = stt_["n"]
            stt_["n"] += 1
            pl = pooled[n % 8]
            S.stt("dve", pl.v, src[:, 16:16 + GT], 1.0 / wdw, pre[:, 16:16 + GT], ALU.mult, ALU.subtract)
            if g == 0:
                S.tt("dve", tmp16.v, src[:, 16:32], invc[:, c, :], ALU.mult)
                S.tt("dve", pl[:, 0:16], tmp16.v, pre[:, 16:32], ALU.subtract)
            mixq.append((c, pl, ts_))
        for c2 in range(4):
            ps = fm_chunk(yb, Wk, 512 + c2 * 128)
            psr_ = fm_chunk(yb, Wr, c2 * 128)
            tb_ = tab[c2 % 2]
            S.dma("act", tb_.v, rtv[c2][:, :, ts_])
            S.tt("dve", t1.v, ps.v, tb_[:, 0, :], ALU.mult)
            S.tt("dve", t2.v, psr_.v, tb_[:, 1, :], ALU.mult)
            rq_ = rqst[c2 % 2]
            S.tt("pool", rq_.v, t1.v, t2.v, ALU.add)
            S.dma("sp", rqkv[c2][:, ts_], rq_.v)

    def body2(g):
        b = g % 2
        ts_ = slice(g * GT, (g + 1) * GT)
        yb = y[b]
        for sub in range(4):
            ps = P.nextps()
            for kc in range(8):
                S.matmul(ps.v, yb[:, kc, sub * 128:(sub + 1) * 128], Wk[kc][:, 1024:1536], start=(kc == 0), stop=(kc == 7))
            S.evac(rvst[b][:, sub, :], ps.v)
        S.dma("sp", D_["rv"].rearrange("(n p) f -> p n f", p=128)[:, 4 * g:4 * g + 4, :], rvst[b].v)
        for c in range(4):
            ps = fm_chunk(yb, Wk, 1536 + c * 128)
            S.act(rgst[b][:, c, :], ps.v, AF.Silu)
        S.dma("sp", D_["rg"].rearrange("c p t -> p c t")[:, :, ts_], rgst[b].v)
        while mixq:
            c, pl, tsl = mixq.pop(0)
            psm = P.nextps()
            S.matmul(psm.v, pw[:, c, :], pl.v)
            cs = cst_[c % 2]
            S.ts("dve", cs.v, psm.v, P.pcol[:, PC_PSC + c:PC_PSC + c + 1], None, ALU.mult)
            S.dma("sp", D_["catT"][c * 128:(c + 1) * 128, tsl], cs.v)

    pre_(0)
    norm_(0)
    for g in range(NG):
        if g + 1 < NG:
            pre_(g + 1)
        body1(g)
        if g + 1 < NG:
            norm_(g + 1)
        body2(g)
    S.barrier()


def phase_retention(P, pf=()):
    S, A, D_ = P.S, P.A, P.dram
    pf_issue = prefetch_w(P, pf)
    NCH = 32
    Qd = [A.alloc([64, S_LEN], BF16, "Qd%d" % i) for i in range(2)]
    Kd = [A.alloc([64, S_LEN], BF16, "Kd%d" % i) for i in range(2)]
    sg = [A.alloc([128, S_LEN], BF16, "sg%d" % i) for i in range(2)]
    Vt = [A.alloc([128, NCH, 128], BF16, "rVt%d" % i) for i in range(2)]
    KdT = [A.alloc([128, NCH, 64], BF16, "KdT%d" % i) for i in range(2)]
    Rbf = [A.alloc([64, NCH, 128], BF16, "Rbf%d" % i) for i in range(2)]
    T32 = [[A.alloc([64, 128], F32, "T32%d_%d" % (i, k)) for k in range(2)] for i in range(2)]
    c01 = P.c01
    St = [A.alloc([128, 4, 128], BF16, "St%d" % i) for i in range(3)]
    sqb = [A.alloc([128, GT], BF16, "sqb%d" % i) for i in range(2)]
    rs = [A.alloc([128, GT], F32, "rs%d" % i) for i in range(2)]
    y1 = [A.alloc([128, GT], F32, "y1%d" % i) for i in range(2)]
    yst = [A.alloc([128, GT], BF16, "yst%d" % i) for i in range(2)]
    rvv = D_["rv"].rearrange("(n p) f -> p n f", p=128)
    ident = P.cbf[0:64, 0:64]
    st = {"nep": 0, "nst": 0}

    def loads(h):
        b = h % 2
        S.dma("sp", Kd[b].v, D_["rqk"][4 + h])
        S.dma("sp", Vt[b].v, rvv[:, :, h * 128:(h + 1) * 128])
        S.dma("sp", Qd[b].v, D_["rqk"][h])
        S.dma("sp", sg[b].v, D_["rg"][h])

    def pre_transposes(h):
        b = h % 2
        for q in range(4):
            pst = P.psr(6, 7)
            pstb = V(pst, pst.ap.bitcast(BF16))
            for r in range(8):
                c = q * 8 + r
                S.transpose(pstb[:, r * 64:(r + 1) * 64], Kd[b][0:64, c * 128:(c + 1) * 128], ident)
            S.evac(KdT[b][:, q * 8:(q + 1) * 8, :], V(pst, pst.ap.bitcast(BF16)[:, 0:512].rearrange("p (a b) -> p a b", a=8)))

    def pre_scan(h, q):
        b = h % 2
        g128 = RET_GAMMA[h] ** 128
        psd = P.psr(4, 6)
        for r in range(4):
            c = q * 4 + r
            if c < NCH - 1:
                S.matmul(psd[0:64, r * 128:(r + 1) * 128], KdT[b][:, c, :], Vt[b][:, c, :])
        for r in range(4):
            c = q * 4 + r
            if c == NCH - 1:
                continue
            tn, tp = T32[b][c % 2], T32[b][(c + 1) % 2]
            if c == 0:
                S.copy("dve", tn.v, psd[0:64, 0:128])
            else:
                S.stt("dve", tn.v, tp.v, g128, psd[0:64, r * 128:(r + 1) * 128], ALU.mult, ALU.add)
            S.ts("dve", Rbf[b][:, c + 1, :], tn.v, g128, None, ALU.mult)

    def main(h, hook):
        b = h % 2

        def stage_a(it):
            G = it["G"]
            pss = P.psr(2, 4)
            for r in range(4):
                c = 4 * G + r
                cs = slice(c * 128, (c + 1) * 128)
                S.matmul(pss[:, r * 128:(r + 1) * 128], Kd[b][:, cs], Qd[b][:, cs])
            st_ = St[st["nst"] % 3]
            st["nst"] += 1
            S.tt("dve", st_.v, V(pss, pss.ap.rearrange("p (a b) -> p a b", a=4)),
                 V(c01, c01.ap.unsqueeze(1).broadcast_to([128, 4, 128])), ALU.mult)
            it["st"] = st_
            if hook is not None:
                hook(G)

        def stage_b(it):
            G, st_ = it["G"], it["st"]
            po = P.psr(0, 2)
            for r in range(4):
                c = 4 * G + r
                cs = slice(c * 128, (c + 1) * 128)
                S.matmul(po[:, r * 128:(r + 1) * 128], Vt[b][:, c, :], st_[:, r, :], start=True, stop=(c == 0), inc=True)
                if c > 0:
                    S.matmul(po[:, r * 128:(r + 1) * 128], Rbf[b][:, c, :], Qd[b][:, cs], start=False, stop=True, inc=True)

            def epilogue(po=po, G=G):
                k = st["nep"] = st["nep"] + 1
                k %= 2
                S.act(sqb[k].v, po.v, AF.Square)
                pn = P.psr(7, 8)
                S.matmul(pn.v, P.cbf[:, 256:384], sqb[k].v)
                S.act(rs[k].v, pn.v, AF.Ln, bias=P.epsc[:, 0:1])
                S.act(rs[k].v, rs[k].v, AF.Exp, scale=-0.5)
                S.copy("act", y1[k].v, po.v)

                def epilogue_b():
                    S.tt("dve", y1[k].v, y1[k].v, rs[k].v, ALU.mult)
                    S.stt("dve", yst[k].v, y1[k].v, P.pcol[:, PC_RNG + h:PC_RNG + h + 1], sg[b][:, G * GT:(G + 1) * GT],
                          ALU.mult, ALU.mult)
                    S.dma("sp", D_["catT"][512 + h * 128:512 + (h + 1) * 128, G * GT:(G + 1) * GT], yst[k].v)
                return epilogue_b
            return epilogue

        pipeline([{"G": G} for G in range(NG)], stage_a, stage_b, 1, defer=1)

    loads(0)
    pf_issue(after=sg[0].last_w + Vt[0].last_w + Qd[0].last_w)
    pre_transposes(0)
    for q in range(8):
        pre_scan(0, q)
    sched = {2: [], 3: [0, 1], 4: [2, 3], 5: [4, 5], 6: [6], 7: [7]}
    for h in range(4):
        hook = None
        if h + 1 < 4:
            loads(h + 1)

            def hook(G, h=h):
                if G == 2:
                    pre_transposes(h + 1)
                for q in sched.get(G, []):
                    pre_scan(h + 1, q)
        main(h, hook)
    S.barrier()


_NC_CACHE = {}


def kernel(**inputs):
    inp = {k: np.asarray(v) for k, v in inputs.items()}
    if "nc" not in _NC_CACHE:
        _NC_CACHE["nc"] = build_program()
    nc = _NC_CACHE["nc"]
    shared = make_inputs(inp, 0)
    in_maps = []
    for b in range(8):
        m = dict(shared)
        m["xT"] = np.ascontiguousarray(inp["x"][b].T)
        in_maps.append(m)
    res = run_bass_kernel_spmd(nc, in_maps, core_ids=list(range(8)))
    out = np.stack([np.asarray(res.results[b]["outT"], np.float32).T for b in range(8)], axis=0)
    return np.ascontiguousarray(out.astype(np.float32))
```

```python
import contextlib
import math
import numpy as np
import concourse.bass as bass
import concourse.mybir as mybir
from concourse.bass_utils import run_bass_kernel_spmd

F32 = mybir.dt.float32
BF16 = mybir.dt.bfloat16
AF = mybir.ActivationFunctionType
ALU = mybir.AluOpType
AX = mybir.AxisListType

N_DMA_SEMS = 64
S_LEN = 4096
D = 1024
NG = 8
GT = 512
EPS = 1e-6
PARTS = set('nqkvbog')
ARRIVAL_ORDER = True
BIG = 30000.0


class T:
    def __init__(self, ap, name=""):
        self.ap = ap
        self.name = name
        self.last_w = []
        self.readers = []

    def __getitem__(self, idx):
        return V(self, self.ap[idx])

    @property
    def v(self):
        return V(self, self.ap)

    def sub(self, idx, name=""):
        return T(self.ap[idx], name or self.name)


class V:
    def __init__(self, t, ap):
        self.t = t
        self.ap = ap

    def __getitem__(self, idx):
        return V(self.t, self.ap[idx])


def _ap(x):
    return x.ap if isinstance(x, (V, T)) else x


class Sched:
    ENGS = ("pe", "act", "dve", "pool", "sp")

    def __init__(self, nc, stack):
        self.nc = nc
        self.q = {e: [] for e in self.ENGS}
        self.cnt = {e: 0 for e in self.ENGS}
        self.seen = {e: {} for e in self.ENGS}
        self.dcnt = [0] * N_DMA_SEMS
        self.dnext = 0
        self.semkey = {}
        for e in self.ENGS:
            self.semkey[("e", e)] = stack.enter_context(nc.semaphore("s_" + e))
        for i in range(N_DMA_SEMS):
            self.semkey[("d", i)] = stack.enter_context(nc.semaphore("d%d" % i))
        self.rr = 0

    def _deps(self, reads, writes):
        deps = []
        for t in reads:
            deps += t.last_w
        for t in writes:
            deps += t.last_w
            deps += t.readers
        return deps

    def _commit(self, reads, writes, ticket):
        for t in reads:
            t.readers.append(ticket)
            if len(t.readers) > 48:
                best = {}
                for k, v in t.readers:
                    if best.get(k, -1) < v:
                        best[k] = v
                t.readers = list(best.items())
        for t in writes:
            best = dict(t.last_w)
            if best.get(ticket[0], -1) < ticket[1]:
                best[ticket[0]] = ticket[1]
            t.last_w = list(best.items())
            t.readers = []

    def _waits(self, eng, deps, skip_self=False):
        best = {}
        for k, v in deps:
            if skip_self and k == ("e", eng):
                continue
            if best.get(k, -1) < v:
                best[k] = v
        out = []
        seen = self.seen[eng]
        for k, v in best.items():
            if seen.get(k, -1) >= v:
                continue
            seen[k] = v
            out.append((k, v))
        return out

    def op(self, eng, fn, reads=(), writes=(), inc=True, skip_self=False):
        reads = [r.t if isinstance(r, V) else r for r in reads if isinstance(r, (V, T))]
        writes = [w.t if isinstance(w, V) else w for w in writes if isinstance(w, (V, T))]
        waits = self._waits(eng, self._deps(reads, writes), skip_self=skip_self)
        if inc:
            self.cnt[eng] += 1
            ticket = (("e", eng), self.cnt[eng])
        else:
            ticket = (("e", eng), self.cnt[eng] + 1)
        self.q[eng].append((waits, fn, (("e", eng), 1) if inc else None))
        self._commit(reads, writes, ticket)
        return ticket

    def dma(self, eng, out, in_, after=(), **kw):
        r = [in_.t] if isinstance(in_, V) else []
        w = [out.t] if isinstance(out, V) else []
        i = self.dnext
        self.dnext = (self.dnext + 1) % N_DMA_SEMS
        deps = self._deps(r, w) + list(after)
        if self.dcnt[i] > 0:
            deps.append((("d", i), self.dcnt[i]))
        waits = self._waits(eng, deps)
        self.dcnt[i] += 16
        ticket = (("d", i), self.dcnt[i])
        o, s = _ap(out), _ap(in_)
        self.q[eng].append((waits, (lambda e, o=o, s=s, kw=kw: e.dma_start(out=o, in_=s, **kw)), (("d", i), 16)))
        self._commit(r, w, ticket)
        return ticket

    def barrier(self):
        deps = [(("e", e), self.cnt[e]) for e in self.ENGS if self.cnt[e] > 0]
        deps += [(("d", i), self.dcnt[i]) for i in range(N_DMA_SEMS) if self.dcnt[i] > 0]
        for e in self.ENGS:
            waits = self._waits(e, deps)
            if waits:
                self.q[e].append((waits, None, None))

    def matmul(self, out, lhsT, rhs, start=True, stop=True, inc=None):
        o, l, r = _ap(out), _ap(lhsT), _ap(rhs)
        if inc is None:
            inc = stop
        fn = lambda e: e.matmul(o, l, r, start=start, stop=stop)
        if start:
            t = self.op("pe", fn, reads=[lhsT, rhs], writes=[out], inc=inc, skip_self=True)
        else:
            t = self.op("pe", fn, reads=[lhsT, rhs], writes=[], inc=inc, skip_self=True)
        if not stop:
            t = (t[0], self.cnt["pe"] + 1) if not inc else t
        best = dict(out.t.last_w)
        if best.get(t[0], -1) < t[1]:
            best[t[0]] = t[1]
        out.t.last_w = list(best.items())
        return t

    def transpose(self, out, in_, ident):
        o, i, d = _ap(out), _ap(in_), _ap(ident)
        return self.op("pe", lambda e: e.transpose(o, i, d), reads=[in_, ident], writes=[out], skip_self=True)

    def act(self, out, in_, func, bias=None, scale=None):
        o, i = _ap(out), _ap(in_)
        kw = {}
        if bias is not None:
            kw["bias"] = _ap(bias)
        if scale is not None:
            kw["scale"] = _ap(scale)
        return self.op("act", lambda e: e.activation(o, i, func, **kw), reads=[in_, bias, scale], writes=[out])

    def tt(self, eng, out, in0, in1, op):
        o, a, b = _ap(out), _ap(in0), _ap(in1)
        return self.op(eng, lambda e: e.tensor_tensor(o, a, b, op), reads=[in0, in1], writes=[out])

    def ts(self, eng, out, in0, s1, s2, op0, op1=None):
        o, a, x1, x2 = _ap(out), _ap(in0), _ap(s1), _ap(s2)
        if op1 is None:
            fn = lambda e: e.tensor_scalar(o, a, x1, None, op0)
        else:
            fn = lambda e: e.tensor_scalar(o, a, x1, x2, op0, op1)
        return self.op(eng, fn, reads=[in0, s1, s2], writes=[out])

    def stt(self, eng, out, in0, scalar, in1, op0, op1):
        o, a, sc, b = _ap(out), _ap(in0), _ap(scalar), _ap(in1)
        return self.op(eng, lambda e: e.scalar_tensor_tensor(o, a, sc, b, op0, op1), reads=[in0, in1, scalar], writes=[out])

    def copy(self, eng, out, in_):
        o, i = _ap(out), _ap(in_)
        if eng == "act":
            return self.op(eng, lambda e: e.copy(o, i), reads=[in_], writes=[out])
        if eng == "dve":
            return self.op(eng, lambda e: e.tensor_scalar(o, i, 1.0, None, ALU.mult), reads=[in_], writes=[out])
        return self.op(eng, lambda e: e.tensor_copy(o, i), reads=[in_], writes=[out])

    def memset(self, eng, out, val):
        o = _ap(out)
        return self.op(eng, lambda e: e.memset(o, val), reads=[], writes=[out])

    def recip(self, out, in_):
        o, i = _ap(out), _ap(in_)
        return self.op("dve", lambda e: e.reciprocal(o, i), reads=[in_], writes=[out])

    def evac(self, out, in_, scale=None):
        self.rr ^= 1
        if self.rr:
            if scale is None:
                return self.copy("act", out, in_)
            return self.act(out, in_, AF.Copy, scale=scale)
        if scale is None:
            return self.copy("dve", out, in_)
        return self.ts("dve", out, in_, scale, None, ALU.mult)

    def emit(self):
        nc = self.nc
        engmap = {"pe": "tensor", "act": "scalar", "dve": "vector", "pool": "gpsimd", "sp": "sync"}
        with nc.Block() as block:
            for e in self.ENGS:
                q = self.q[e]
                if not q:
                    continue

                def body(engine, q=q):
                    for waits, fn, inc in q:
                        for k, v in waits:
                            engine.wait_ge(self.semkey[k], v)
                        if fn is None:
                            continue
                        ins = fn(engine)
                        if inc is not None:
                            ins.then_inc(self.semkey[inc[0]], inc[1])

                getattr(block, engmap[e])(body)


class Arena:
    def __init__(self, ap, nbytes):
        self.ap = ap
        self.cap = nbytes
        self.off = 0

    def alloc(self, shape, dt, name=""):
        esz = 4 if dt == F32 else 2
        nfree = 1
        for s in shape[1:]:
            nfree *= s
        nb = nfree * esz
        o = self.off
        self.off += (nb + 31) // 32 * 32
        assert self.off <= self.cap, "SBUF arena overflow: %d > %d (%s)" % (self.off, self.cap, name)
        v = self.ap[0:shape[0], o // 2:o // 2 + nb // 2]
        if dt == F32:
            v = v.bitcast(F32)
        if len(shape) == 3:
            v = v.rearrange("p (a b) -> p a b", a=shape[1])
        elif len(shape) == 4:
            v = v.rearrange("p (a b c) -> p a b c", a=shape[1], b=shape[2])
        return T(v, name)


PC_G = 0
PC_MCW = 40
PC_MCB = 72
PC_MNG = 80
PC_FCW = (84, 260)
PC_FCB = (216, 392)
PC_PSC = 436
PC_RNG = 440
PC_N = 444


def _cols(vec, nchunk):
    return np.ascontiguousarray(np.asarray(vec, np.float32).reshape(nchunk, 128).T)


def make_pcol(inp):
    pc = np.zeros((128, PC_N), np.float32)
    gs = [inp["norm_mix_g"][0], inp["norm_ffn_g"][0], inp["norm_mix_g"][1], inp["norm_ffn_g"][1], inp["norm_out_g"]]
    for n, g in enumerate(gs):
        pc[:, PC_G + n * 8:PC_G + (n + 1) * 8] = _cols(g, 8)
    cw = inp["mlstm_conv_w"][0]
    for j in range(4):
        pc[:, PC_MCW + j:PC_MCW + 32:4] = _cols(cw[j], 8)
    pc[:, PC_MCB:PC_MCB + 8] = _cols(inp["mlstm_conv_b"][0], 8)
    pc[:, PC_MNG:PC_MNG + 4] = _cols(inp["mlstm_norm_g"][0], 4)
    for l in range(2):
        fw = inp["ffn_conv_w"][l]
        for j in range(3):
            pc[:, PC_FCW[l] + j:PC_FCW[l] + 132:3] = _cols(fw[j], 44)
        pc[:, PC_FCB[l]:PC_FCB[l] + 44] = _cols(inp["ffn_conv_b"][l], 44)
    pc[:, PC_PSC:PC_PSC + 4] = _cols(inp["pool_scale"][0], 4)
    pc[:, PC_RNG:PC_RNG + 4] = _cols(inp["ret_norm_g"][0], 4)
    return pc


def make_cst():
    c = np.zeros((128, 512), np.float32)
    c[:, 0:128] = np.eye(128, dtype=np.float32)
    c[:, 128:256] = 1.0 / 1024.0
    c[:, 256:384] = 1.0 / 128.0
    c[:, 384:512] = 1.0
    return c


class Ctx:
    pass


def pipeline(items, stage_a, stage_b, skew, defer=3, hooks=()):
    n = len(items)
    pending = []
    hk = sorted([(min(n - 1, int(n * f)), i, fn) for i, (f, fn) in enumerate(hooks)])
    for t in range(n + skew):
        if t < n:
            stage_a(items[t])
        while hk and hk[0][0] <= t:
            hk.pop(0)[2]()
        if t >= skew:
            ep = stage_b(items[t - skew])
            pending = [(c - 1, f) for c, f in pending]
            due = [x for x in pending if x[0] <= 0]
            pending = [x for x in pending if x[0] > 0]
            for c, f in due:
                nxt = f()
                if nxt is not None:
                    pending.append((defer, nxt))
            if ep is not None:
                pending.append((defer, ep))
    while pending:
        c, f = pending.pop(0)
        nxt = f()
        if nxt is not None:
            pending.append((0, nxt))


def rmsnorm_group(P, xg, nidx, y, w=GT):
    S = P.S
    psms = P.nextps()
    for kc in range(8):
        sq = P.sq[kc % 2]
        S.act(sq[:, 0:w], xg[:, kc, 0:w], AF.Square)
        S.matmul(psms[:, 0:w], P.cbf[:, 128:256], sq[:, 0:w], start=(kc == 0), stop=(kc == 7), inc=True)
    S.act(P.rstd[:, 0:w], psms[:, 0:w], AF.Ln, bias=P.epsc[:, 0:1])
    S.act(P.rstd[:, 0:w], P.rstd[:, 0:w], AF.Exp, scale=-0.5)
    for kc in range(8):
        S.stt("dve", y[:, kc, 0:w], xg[:, kc, 0:w], P.pcol[:, PC_G + nidx * 8 + kc:PC_G + nidx * 8 + kc + 1],
              P.rstd[:, 0:w], ALU.mult, ALU.mult)


def phase_proj_ab(P):
    S, A, D_ = P.S, P.A, P.dram
    A.off = P.abase
    Wk = [A.alloc([128, 3592], BF16, "wab%d" % kc) for kc in range(8)]
    xg = [A.alloc([128, 8, GT], F32, "xg%d" % i) for i in range(2)]
    y = [A.alloc([128, 8, GT], BF16, "y%d" % i) for i in range(2)]
    qst = [A.alloc([64, 8, GT], BF16, "qst%d" % i) for i in range(2)]
    kst = [A.alloc([64, 8, GT], BF16, "kst%d" % i) for i in range(2)]
    avst = [A.alloc([128, 4, 512], BF16, "avst%d" % i) for i in range(2)]
    bvst = [A.alloc([128, 4, 512], BF16, "bvst%d" % i) for i in range(2)]
    bqkst = [A.alloc([128, 8, GT], BF16, "bqkst%d" % i) for i in range(2)]
    bost = [A.alloc([128, 4, GT], BF16, "bost%d" % i) for i in range(2)]
    gst = [A.alloc([8, GT], F32, "gst%d" % i) for i in range(2)]
    pre = [A.alloc([128, 3 + GT], F32, "pre%d" % i) for i in range(3)]
    acc = [A.alloc([128, GT], F32, "acc%d" % i) for i in range(3)]
    tail = [A.alloc([128, 4], F32, "tail%d" % c) for c in range(8)]
    for c in range(8):
        S.memset("dve", tail[c].v, 0.0)
    xTv = D_["xT"].rearrange("(kc p) t -> p kc t", p=128)
    stt_ = {"npre": 0}

    def pre_(g):
        S.dma("act", xg[g % 2].v, xTv[:, :, g * GT:(g + 1) * GT])

    def norm_(g):
        rmsnorm_group(P, xg[g % 2], 0, y[g % 2])

    def fm_chunk(yb, col0, m=128):
        ps = P.nextps()
        for kc in range(8):
            S.matmul(ps[0:m, :], Wk[kc][:, col0:col0 + m], yb[:, kc, :], start=(kc == 0), stop=(kc == 7))
        return ps

    def body1(g):
        b = g % 2
        ts_ = slice(g * GT, (g + 1) * GT)
        yb = y[b]
        if g == 0 and ARRIVAL_ORDER:
            for kc in range(8):
                for i in range(8):
                    S.matmul(P.ps[i].v, Wk[kc][:, i * 128:(i + 1) * 128], yb[:, kc, :], start=(kc == 0), stop=(kc == 7),
                             inc=(kc == 7))
        for c in range(4):
            ps = P.ps[c] if (g == 0 and ARRIVAL_ORDER) else fm_chunk(yb, c * 128)
            S.evac(qst[b][:, 2 * c, :], ps[0:64, :], scale=0.125)
            S.evac(qst[b][:, 2 * c + 1, :], ps[64:128, :], scale=0.125)
        S.dma("sp", D_["qa"].rearrange("h p t -> p h t")[:, :, ts_], qst[b].v)
        for c in range(4):
            ps = P.ps[4 + c] if (g == 0 and ARRIVAL_ORDER) else fm_chunk(yb, 512 + c * 128)
            S.evac(kst[b][:, 2 * c, :], ps[0:64, :])
            S.evac(kst[b][:, 2 * c + 1, :], ps[64:128, :])
        S.dma("sp", D_["ka"].rearrange("h p t -> p h t")[:, :, ts_], kst[b].v)
        for (col0, st, dname) in ((1024, avst[b], "va"), (2560, bvst[b], "bv")):
            for sub in range(4):
                ps = P.nextps()
                for kc in range(8):
                    S.matmul(ps.v, yb[:, kc, sub * 128:(sub + 1) * 128], Wk[kc][:, col0:col0 + 512],
                             start=(kc == 0), stop=(kc == 7))
                S.evac(st[:, sub, :], ps.v)
            S.dma("sp", D_[dname].rearrange("(n p) f -> p n f", p=128)[:, 4 * g:4 * g + 4, :], st.v)

    def body2(g):
        b = g % 2
        ts_ = slice(g * GT, (g + 1) * GT)
        yb = y[b]
        prev = None
        for c in range(8):
            ps = fm_chunk(yb, 1536 + c * 128)
            pr = pre[stt_["npre"] % 3]
            ac = acc[stt_["npre"] % 3]
            stt_["npre"] += 1
            S.copy("act", pr[:, 3:3 + GT], ps.v)
            S.copy("pool", pr[:, 0:3], tail[c][:, 0:3])
            S.copy("pool", tail[c][:, 0:3], pr[:, GT:GT + 3])
            cw = PC_MCW + c * 4
            S.act(ac.v, ps.v, AF.Identity, bias=P.pcol[:, PC_MCB + c:PC_MCB + c + 1],
                  scale=P.pcol[:, cw + 3:cw + 4])
            for j in range(3):
                S.stt("dve", ac.v, pr[:, j:j + GT], P.pcol[:, cw + j:cw + j + 1], ac.v, ALU.mult, ALU.add)
            if prev is not None:
                S.act(bqkst[b][:, prev[0], :], prev[1].v, AF.Silu)
            prev = (c, ac)
        S.act(bqkst[b][:, prev[0], :], prev[1].v, AF.Silu)
        S.dma("sp", D_["bqk"].rearrange("c p t -> p c t")[:, :, ts_], bqkst[b].v)
        for c in range(4):
            ps = fm_chunk(yb, 3072 + c * 128)
            S.act(bost[b][:, c, :], ps.v, AF.Sigmoid)
        S.dma("sp", D_["bo"].rearrange("c p t -> p c t")[:, :, ts_], bost[b].v)
        ps = fm_chunk(yb, 3584, m=8)
        S.copy("dve", gst[b].v, ps[0:8, :])
        S.dma("sp", D_["gates"][:, ts_], gst[b].v)

    pre_(0)
    for kc in range(8):
        S.dma("pool", Wk[kc].v, D_["w_in_ab"][kc * 128:(kc + 1) * 128, :], after=xg[0].last_w)
    norm_(0)
    for g in range(NG):
        if g + 1 < NG:
            pre_(g + 1)
        body1(g)
        if g + 1 < NG:
            norm_(g + 1)
        body2(g)
    S.barrier()


def build_program(upto=99, debug=()):
    nc = bass.Bass("TRN2", target_bir_lowering=False)
    P = Ctx()
    P.nc = nc
    dram = {}

    def din(name, shape):
        dram[name] = nc.dram_tensor(name, list(shape), F32, kind="ExternalInput").ap()

    def dscr(name, shape, dt):
        kind = "ExternalOutput" if name in debug else "Internal"
        dram[name] = nc.dram_tensor(name, list(shape), dt, kind=kind).ap()

    din("xT", (D, S_LEN))
    din("cst", (128, 512))
    din("pcol", (128, PC_N))
    din("w_in_ab", (D, 3592))
    dscr("qa", (8, 64, S_LEN), BF16)
    dscr("ka", (8, 64, S_LEN), BF16)
    dscr("va", (S_LEN, 512), BF16)
    dscr("bv", (S_LEN, 512), BF16)
    dscr("bqk", (8, 128, S_LEN), BF16)
    dscr("bo", (4, 128, S_LEN), BF16)
    dscr("gates", (8, S_LEN), F32)
    din("tb", (128, 2048))
    din("b31", (128, 8))
    din("sqm", (128, 256))
    din("mobac", (128, 1024))
    din("blk1h", (16, S_LEN))
    dscr("catT", (D, S_LEN), BF16)
    din("gb", (4, 2))
    dscr("grow", (8, S_LEN), F32)
    dscr("erow", (4, 32), F32)
    if "dbgC" in debug:
        dscr("dbgC", (128, 32 * 130), BF16)
        dscr("dbgK", (128, 32 * 128), BF16)
        dscr("dbgV", (128, 32 * 130), BF16)
        dscr("dbgE", (128, 32), F32)
        dscr("dbgQ", (128, S_LEN), BF16)
    din("w_out_ab", (D, D))
    din("w_out_cd", (D, D))
    for l in range(2):
        din("ffn_w_up%d" % l, (D, 5632))
        din("ffn_w_down%d" % l, (2816, D))
    dscr("h1T", (D, S_LEN), F32)
    din("w_in_cd", (D, 2048))
    din("w_rot", (D, 512))
    din("rtab", (512, 2 * S_LEN))
    din("invc", (128, 64))
    din("pool_w", (512, 128))
    dscr("rqk", (8, 64, S_LEN), BF16)
    dscr("rv", (S_LEN, 512), BF16)
    dscr("rg", (4, 128, S_LEN), BF16)
    dscr("hT", (D, S_LEN), F32)
    dscr("aT", (2816, S_LEN), BF16)
    dram["outT"] = nc.dram_tensor("outT", [D, S_LEN], F32, kind="ExternalOutput").ap()
    P.dram = dram

    with contextlib.ExitStack() as st:
        S = Sched(nc, st)
        P.S = S
        NBYTES = 207 * 1024
        arena_ap = st.enter_context(nc.sbuf_tensor("arena", [128, NBYTES // 2], BF16)).ap()
        A = Arena(arena_ap, NBYTES)
        P.A = A
        P.ps = [T(st.enter_context(nc.psum_tensor("ps%d" % i, [128, 512], F32)).ap(), "ps%d" % i) for i in range(8)]
        P.psi = 0
        P.pref = {}
        P.pending_w = []

        def nextps():
            P.psi = (P.psi + 1) % 8
            return P.ps[P.psi]
        P.nextps = nextps
        P.psrc = {}

        def psr(lo, hi):
            i = P.psrc.get((lo, hi), lo)
            P.psrc[(lo, hi)] = lo + (i + 1 - lo) % (hi - lo)
            return P.ps[i]
        P.psr = psr
        P.cbf = A.alloc([128, 512], BF16, "cbf")
        S.dma("pool", P.cbf.v, dram["cst"])
        P.pcol = A.alloc([128, PC_N], F32, "pcol")
        S.dma("sp", P.pcol.v, dram["pcol"])
        P.c01 = A.alloc([128, 128], BF16, "c01")
        S.dma("pool", P.c01.v, dram["sqm"][:, 128:256])
        P.epsc = A.alloc([128, 2], F32, "epsc")
        S.memset("dve", P.epsc.v, EPS)
        P.sq = [A.alloc([128, GT], BF16, "sq%d" % i) for i in range(2)]
        P.rstd = A.alloc([128, GT], F32, "rstd")
        P.abase = A.off

        if upto >= 1:
            phase_proj_ab(P)
        if upto >= 2:
            phase_moba(P)
        if upto >= 3:
            phase_mlstm(P, pf=[("wo", dram["w_out_ab"], 8, 1024), ("wu", dram["ffn_w_up0"], 2, 5632)])
        if upto >= 4:
            phase_outproj_ffnup(P, 0, "xT", "w_out_ab", "ffn_w_up0")
        if upto >= 5:
            phase_ffndown(P, 0, "ffn_w_down0", final=False,
                          pf=[("wcd", dram["w_in_cd"], 8, 2048), ("wrot", dram["w_rot"], 8, 512)])
        if upto >= 6:
            phase_proj_cd(P)
        if upto >= 7:
            phase_retention(P, pf=[("wo", dram["w_out_cd"], 8, 1024), ("wu", dram["ffn_w_up1"], 7, 5632)])
        if upto >= 8:
            phase_outproj_ffnup(P, 1, "hT", "w_out_cd", "ffn_w_up1")
        if upto >= 9:
            phase_ffndown(P, 1, "ffn_w_down1", final=True)
        S.barrier()
        S.emit()
    return nc


def _rel_bucket_np(d):
    d = np.maximum(d, 0)
    df = np.maximum(d, 1).astype(np.float32)
    large = 16 + (np.log(df / np.float32(16)) / np.float32(math.log(128 / 16)) * np.float32(16)).astype(np.int32)
    large = np.minimum(large, 31)
    return np.where(d < 16, d, large)


def make_moba_consts(rel_bias):
    k = np.arange(128)[:, None]
    q = np.arange(128)[None, :]
    tb = np.zeros((128, 8, 2, 128), np.float32)
    for di, delta in enumerate((0, 128)):
        idx = _rel_bucket_np(q - k + delta)
        for h in range(8):
            tb[:, h, di, :] = rel_bias[idx, h]
    b31 = np.ascontiguousarray(np.broadcast_to(rel_bias[31][None, :], (128, 8))).astype(np.float32)
    caus = np.where(q >= k, 0.0, -BIG).astype(np.float32)
    c01 = (q >= k).astype(np.float32)
    i = np.arange(32)[:, None]
    n = np.arange(16)[None, :]
    cur = i // 2
    pastm = np.where(n < cur, 0.0, -1e30).astype(np.float32)
    cmask = np.where(n == cur, 0.0, -BIG).astype(np.float32)
    mobac = np.zeros((128, 2, 512), np.float32)
    mobac[:, 0, :] = pastm.reshape(1, 512)
    mobac[:, 1, :] = cmask.reshape(1, 512)
    blk1h = (np.arange(4096)[None, :] // 256 == np.arange(16)[:, None]).astype(np.float32)
    sqm = np.concatenate([caus, c01], axis=1)
    return {"tb": tb.reshape(128, 8 * 2 * 128), "b31": b31, "sqm": sqm, "mobac": mobac.reshape(128, 1024), "blk1h": blk1h}


def phase_moba(P):
    S, A, D_ = P.S, P.A, P.dram
    A.off = P.abase
    Qa = [A.alloc([80, S_LEN], BF16, "Qa%d" % i) for i in range(2)]
    Ka = [A.alloc([80, S_LEN], BF16, "Ka%d" % i) for i in range(2)]
    Vt = [A.alloc([128, 32, 65], BF16, "Vt%d" % i) for i in range(2)]
    tb = A.alloc([128, 8, 2, 128], F32, "tb")
    b31 = A.alloc([128, 8], F32, "b31")
    sqm = A.alloc([128, 256], F32, "sqm")
    mobac = A.alloc([128, 2, 512], F32, "mobac")
    Tb = [A.alloc([128, 2, 128], BF16, "Tb%d" % i) for i in range(2)]
    km32 = A.alloc([64, 16], F32, "km32")
    kmb = A.alloc([64, 16], BF16, "kmb")
    gm = A.alloc([128, 32, 16], F32, "gm")
    sel = A.alloc([128, 32, 16], F32, "sel")
    top8 = A.alloc([128, 32, 8], F32, "top8")
    thr = A.alloc([128, 32, 1], F32, "thr")
    M80 = A.alloc([128, 32, 80], BF16, "M80")
    PT = [A.alloc([128, GT], BF16, "PT%d" % i) for i in range(5)]
    rd = [A.alloc([65, GT], F32, "rd%d" % i) for i in range(2)]
    ones32 = A.alloc([65, 64], F32, "ones32")
    bcs = [A.alloc([64, GT], F32, "bcs%d" % i) for i in range(2)]
    ost = [A.alloc([64, GT], BF16, "ost%d" % i) for i in range(2)]
    S.dma("sp", tb.v, D_["tb"].rearrange("p (h d q) -> p h d q", h=8, d=2))
    S.dma("sp", b31.v, D_["b31"])
    S.dma("sp", sqm.v, D_["sqm"])
    S.dma("sp", mobac.v, D_["mobac"].rearrange("p (a b) -> p a b", a=2))
    S.memset("dve", M80.v, 0.0)
    S.memset("dve", ones32.v, 1.0)
    for i in range(2):
        S.dma("pool", Ka[i][64:80, :], D_["blk1h"])
        S.memset("dve", Vt[i][:, :, 64:65], 1.0)
    ident = P.cbf[:, 0:128]
    vav = D_["va"].rearrange("(n p) f -> p n f", p=128)
    st = {"npt": 0, "po": None}

    def mlstm_gate_prep():
        I4 = A.alloc([4, S_LEN], F32, "I4")
        F4 = A.alloc([4, S_LEN], F32, "F4")
        t0 = A.alloc([4, S_LEN], F32, "gt0")
        t1 = A.alloc([4, S_LEN], F32, "gt1")
        t2 = A.alloc([4, S_LEN], F32, "gt2")
        ones4 = A.alloc([4, S_LEN], F32, "ones4")
        gb4 = A.alloc([4, 2], F32, "gb4")
        onec = A.alloc([4, 2], F32, "onec4")
        S.dma("sp", gb4.v, D_["gb"])
        S.dma("sp", I4.v, D_["gates"][0:4, :])
        S.dma("sp", F4.v, D_["gates"][4:8, :])
        S.memset("dve", ones4.v, 1.0)
        S.memset("dve", onec.v, 1.0)
        lnscale = math.log(128.0 ** -0.5)
        S.act(t0.v, F4.v, AF.Abs, bias=gb4[:, 1:2])
        S.act(t0.v, t0.v, AF.Exp, scale=-1.0)
        S.act(t0.v, t0.v, AF.Ln, bias=onec[:, 0:1])
        S.ts("dve", t1.v, F4.v, gb4[:, 1:2], 0.0, ALU.add, ALU.min)
        S.tt("dve", t1.v, t1.v, t0.v, ALU.subtract)
        S.op("dve", lambda e, o=t2.ap, d0=ones4.ap, d1=t1.ap: e.tensor_tensor_scan(o, d0, d1, 0.0, ALU.mult, ALU.add),
             reads=[ones4, t1], writes=[t2])
        S.ts("dve", t0.v, I4.v, gb4[:, 0:1], lnscale, ALU.add, ALU.add)
        S.tt("dve", t0.v, t0.v, t2.v, ALU.subtract)
        t3 = A.alloc([4, S_LEN], F32, "gt3")
        er = A.alloc([4, 32], F32, "ger")
        t2v = V(t2, t2.ap.rearrange("p (c i) -> p c i", i=128))
        t3v = V(t3, t3.ap.rearrange("p (c i) -> p c i", i=128))
        S.memset("dve", t3[:, 0:128], 0.0)
        S.copy("dve", t3v[:, 1:32, :], V(t2, t2v.ap[:, 0:31, 127:128].broadcast_to([4, 31, 128])))
        S.tt("dve", t1.v, t2.v, t3.v, ALU.subtract)
        S.act(t1.v, t1.v, AF.Exp)
        S.tt("dve", t0.v, t0.v, t3.v, ALU.add)
        S.act(t0.v, t0.v, AF.Exp)
        S.tt("dve", er[:, 1:32], V(t2, t2v.ap[:, 1:32, 127]), V(t2, t2v.ap[:, 0:31, 127]), ALU.subtract)
        S.copy("dve", er[:, 0:1], t2[:, 127:128])
        S.act(er.v, er.v, AF.Exp)
        S.dma("sp", D_["grow"][0:4, :], t1.v)
        S.dma("sp", D_["grow"][4:8, :], t0.v)
        S.dma("sp", D_["erow"], er.v)

    def loads(h):
        b = h % 2
        S.dma("sp", Qa[b][0:64, :], D_["qa"][h])
        S.dma("sp", Ka[b][0:64, :], D_["ka"][h])
        S.dma("sp", Vt[b][:, :, 0:64], vav[:, :, h * 64:(h + 1) * 64])

    def preamble_pieces(h):
        b = h % 2
        stp = {}

        def p1():
            S.stt("dve", Tb[b][:, 0, :], tb[:, h, 0, :], b31[:, h:h + 1], sqm[:, 0:128], ALU.subtract, ALU.add)
            S.ts("dve", Tb[b][:, 1, :], tb[:, h, 1, :], b31[:, h:h + 1], None, ALU.subtract)
            kv = V(Ka[b], Ka[b].ap[0:64, :].rearrange("p (n s) -> p n s", s=256))
            S.op("dve", lambda e, o=km32.ap, i_=kv.ap: e.tensor_reduce(o, i_, AX.X, ALU.add), reads=[kv], writes=[km32])
            S.ts("dve", kmb.v, km32.v, 1.0 / 256.0, None, ALU.mult)

        def p2():
            psg = P.psr(6, 8)
            for i in range(32):
                S.matmul(psg[:, i * 16:(i + 1) * 16], Qa[b][0:64, i * 128:(i + 1) * 128], kmb.v)
            psg3 = V(psg, psg.ap.rearrange("p (a b) -> p a b", a=32))
            S.tt("dve", gm.v, psg3, V(mobac, mobac.ap[:, 0, :].rearrange("p (a b) -> p a b", a=32)), ALU.add)

        def p3():
            for i in range(16):
                S.op("dve", lambda e, o=top8.ap[:, i, :], i_=gm.ap[:, i, :]: e.max(o, i_), reads=[gm], writes=[top8])

        def p4():
            for i in range(16, 32):
                S.op("dve", lambda e, o=top8.ap[:, i, :], i_=gm.ap[:, i, :]: e.max(o, i_), reads=[gm], writes=[top8])
            S.ts("dve", thr.v, top8[:, :, 2:3], -1e29, None, ALU.max)
            S.tt("dve", sel.v, gm.v, V(thr, thr.ap.broadcast_to([128, 32, 16])), ALU.is_ge)
            S.stt("dve", M80[:, :, 64:80], sel.v, BIG, V(mobac, mobac.ap[:, 1, :].rearrange("p (a b) -> p a b", a=32)),
                  ALU.mult, ALU.add)

        def p5(G):
            pst = P.psr(6, 8)
            pstb = V(pst, pst.ap.bitcast(BF16))
            for r in range(4):
                S.transpose(pstb[0:80, r * 128:(r + 1) * 128], M80[:, 4 * G + r, :], ident)
            S.copy("dve", Qa[b][64:80, G * GT:(G + 1) * GT], pstb[64:80, 0:GT])

        return [p1, p2, p3, p4] + [(lambda G=G: p5(G)) for G in range(NG)]

    def preamble(h):
        for f in preamble_pieces(h):
            f()

    def mainloop(h, hooks):
        b = h % 2
        items = []
        for G in range(NG):
            nj = 4 * G + 4
            for j in range(nj):
                items.append({"G": G, "j": j, "first": j == 0, "last": j == nj - 1})

        def stage_a(it):
            G, j = it["G"], it["j"]
            r0 = max(0, j - 4 * G)
            c0 = r0 * 128
            pss = P.psr(2, 6)
            adds = []
            for r in range(r0, 4):
                d = 4 * G + r - j
                if d in (0, 1):
                    adds.append((r, d))
            S.matmul(pss[:, c0:GT], Ka[b][0:80, j * 128:(j + 1) * 128], Qa[b][0:80, G * GT + c0:(G + 1) * GT],
                     start=True, stop=(len(adds) == 0), inc=True)
            for ai, (r, d) in enumerate(adds):
                S.matmul(pss[:, r * 128:(r + 1) * 128], ident, Tb[b][:, d, :], start=False,
                         stop=(ai == len(adds) - 1), inc=True)
            it["pss"], it["c0"] = pss, c0

        def stage_b(it):
            G, j, pss, c0 = it["G"], it["j"], it["pss"], it["c0"]
            if it["first"]:
                st["po"] = P.psr(0, 2)
            po = st["po"]
            pt = PT[st["npt"] % 5]
            st["npt"] += 1
            S.act(pt[:, c0:GT], pss[:, c0:GT], AF.Exp)
            S.matmul(po[0:65, c0:GT], Vt[b][:, j, :], pt[:, c0:GT], start=it["first"], stop=it["last"], inc=True)
            if not it["last"]:
                return None

            def epilogue(po=po, G=G):
                k = st["nep"] = st.get("nep", 0) + 1
                rd_, bcs_ = rd[k % 2], bcs[k % 2]
                S.act(rd_[64:65, :], po[64:65, :], AF.Ln)
                S.act(rd_[64:65, :], rd_[64:65, :], AF.Exp, scale=-1.0)
                pbc = P.psr(6, 8)
                S.matmul(pbc[0:64, :], ones32[64:65, :], rd_[64:65, :])
                S.copy("dve", bcs_.v, pbc[0:64, :])
                o_ = ost[k % 2]
                S.tt("dve", o_.v, po[0:64, :], bcs_.v, ALU.mult)
                S.dma("sp", D_["catT"][h * 64:(h + 1) * 64, G * GT:(G + 1) * GT], o_.v)
            return epilogue

        pipeline(items, stage_a, stage_b, 3, defer=3, hooks=hooks)

    loads(0)
    preamble(0)
    for h in range(8):
        hooks = []
        if h + 1 < 8:
            loads(h + 1)
            pcs = preamble_pieces(h + 1)
            fr = [0.2, 0.3, 0.4, 0.5] + [0.58 + 0.045 * i for i in range(NG)]
            hooks += list(zip(fr, pcs))
        if h == 2:
            hooks.append((0.1, mlstm_gate_prep))
        mainloop(h, hooks)
    S.barrier()


def make_inputs(inp, b):
    im = {"xT": np.ascontiguousarray(inp["x"][b].T), "cst": make_cst(), "pcol": make_pcol(inp),
          "w_in_ab": np.ascontiguousarray(inp["w_in_ab"][0])}
    im.update(make_moba_consts(np.asarray(inp["rel_bias"], np.float32)))
    im["w_out_ab"] = np.ascontiguousarray(inp["w_out_ab"][0])
    im["w_out_cd"] = np.ascontiguousarray(inp["w_out_cd"][0])
    for l in range(2):
        im["ffn_w_up%d" % l] = np.ascontiguousarray(inp["ffn_w_up"][l])
        im["ffn_w_down%d" % l] = np.ascontiguousarray(inp["ffn_w_down"][l])
    im.update(make_cd_consts(inp))
    im["gb"] = np.ascontiguousarray(np.stack([inp["mlstm_b_i"][0], inp["mlstm_b_f"][0]], axis=1).astype(np.float32))
    return im


def phase_mlstm(P, pf=()):
    S, A, D_ = P.S, P.A, P.dram
    pf_issue = prefetch_w(P, pf)
    NCH = 32
    bq = [A.alloc([128, S_LEN], BF16, "bq%d" % i) for i in range(2)]
    bk = [A.alloc([128, S_LEN], BF16, "bk%d" % i) for i in range(2)]
    og = [A.alloc([128, S_LEN], BF16, "og%d" % i) for i in range(2)]
    Vt = [A.alloc([128, NCH, 130], BF16, "mVt%d" % i) for i in range(2)]
    qsc = A.alloc([128, S_LEN], F32, "qsc")
    ksc = A.alloc([128, S_LEN], F32, "ksc")
    kdT = [A.alloc([128, NCH, 128], BF16, "kdT%d" % i) for i in range(2)]
    Cbf = [A.alloc([128, NCH, 130], BF16, "Cbf%d" % i) for i in range(2)]
    T32 = [[A.alloc([128, 130], F32, "T32%d_%d" % (i, k)) for k in range(2)] for i in range(2)]
    ecol = [A.alloc([128, NCH], F32, "ecol%d" % i) for i in range(2)]
    c01 = P.c01
    St = [A.alloc([128, 4, 128], BF16, "St%d" % i) for i in range(3)]
    drow = [A.alloc([1, GT], F32, "drow%d" % i) for i in range(2)]
    e2row = [A.alloc([1, GT], BF16, "e2row%d" % i) for i in range(2)]
    sqb = [A.alloc([128, GT], BF16, "sqb%d" % i) for i in range(2)]
    rs = [A.alloc([128, GT], F32, "rs%d" % i) for i in range(2)]
    y1 = [A.alloc([128, GT], F32, "y1%d" % i) for i in range(2)]
    yst = [A.alloc([128, GT], BF16, "yst%d" % i) for i in range(2)]
    for i in range(2):
        S.memset("dve", Vt[i][:, :, 128:129], 1.0)
        S.memset("dve", Vt[i][:, :, 129:130], 0.0)
    bvv = D_["bv"].rearrange("(n p) f -> p n f", p=128)
    ident = P.cbf[:, 0:128]
    st = {"nst": 0, "nep": 0, "ntmp": 0}
    psd_slots = [P.ps[6].sub((slice(None), slice(k * 130, (k + 1) * 130)), "psd%d" % k) for k in range(3)]
    cm1 = A.alloc([1, 2], F32, "cm1")
    S.memset("dve", cm1[:, 0:1], -1.0)
    S.memset("dve", cm1[:, 1:2], EPS)

    def loads(h):
        b = h % 2
        S.dma("sp", qsc.v, D_["grow"][h:h + 1, :].partition_broadcast(128))
        S.dma("sp", ksc.v, D_["grow"][4 + h:5 + h, :].partition_broadcast(128))
        S.dma("sp", ecol[b].v, D_["erow"][h:h + 1, :].partition_broadcast(128))
        S.dma("sp", bq[b].v, D_["bqk"][h])
        S.dma("sp", bk[b].v, D_["bqk"][4 + h])
        S.dma("sp", Vt[b][:, :, 0:128], bvv[:, :, h * 128:(h + 1) * 128])
        S.dma("sp", og[b].v, D_["bo"][h])

    def pre_scale(h):
        b = h % 2
        for G in range(NG):
            gs = slice(G * GT, (G + 1) * GT)
            S.tt("dve", bq[b][:, gs], bq[b][:, gs], qsc[:, gs], ALU.mult)
            S.tt("dve", bk[b][:, gs], bk[b][:, gs], ksc[:, gs], ALU.mult)
        for q in range(4):
            pst = P.psr(4, 6)
            pstb = V(pst, pst.ap.bitcast(BF16))
            for r in range(8):
                c = q * 8 + r
                S.transpose(pstb[:, r * 128:(r + 1) * 128], bk[b][:, c * 128:(c + 1) * 128], ident)
            S.evac(kdT[b][:, q * 8:(q + 1) * 8, :], V(pst, pst.ap.bitcast(BF16).rearrange("p (a b) -> p a b", a=8)))

    def pre_scan(h, q):
        b = h % 2
        for r in range(4):
            c = q * 4 + r
            if c == NCH - 1:
                continue
            psd = psd_slots[c % 3]
            S.matmul(psd.v, kdT[b][:, c, :], Vt[b][:, c, :])
            tn, tp = T32[b][c % 2], T32[b][(c + 1) % 2]
            if c == 0:
                S.ts("dve", tn.v, psd.v, 1.0, None, ALU.mult)
            else:
                S.stt("dve", tn.v, tp.v, ecol[b][:, c - 1:c], psd.v, ALU.mult, ALU.add)
            S.ts("dve", Cbf[b][:, c + 1, :], tn.v, ecol[b][:, c:c + 1], None, ALU.mult)

    def main(h, hook):
        b = h % 2

        def stage_a(it):
            G = it["G"]
            pss = P.psr(4, 6)
            for r in range(4):
                c = 4 * G + r
                cs = slice(c * 128, (c + 1) * 128)
                S.matmul(pss[:, r * 128:(r + 1) * 128], bk[b][:, cs], bq[b][:, cs])
            st_ = St[st["nst"] % 3]
            st["nst"] += 1
            S.tt("dve", st_.v, V(pss, pss.ap.rearrange("p (a b) -> p a b", a=4)),
                 V(c01, c01.ap.unsqueeze(1).broadcast_to([128, 4, 128])), ALU.mult)
            it["st"] = st_
            if hook is not None:
                hook(G)

        def stage_b(it):
            G, st_ = it["G"], it["st"]
            po = P.psr(0, 2)
            pden = P.psr(2, 4)
            for r in range(4):
                c = 4 * G + r
                cs = slice(c * 128, (c + 1) * 128)
                rs_ = slice(r * 128, (r + 1) * 128)
                S.matmul(po[:, rs_], Vt[b][:, c, 0:128], st_[:, r, :], start=True, stop=(c == 0), inc=True)
                if c > 0:
                    S.matmul(po[:, rs_], Cbf[b][:, c, 0:128], bq[b][:, cs], start=False, stop=True, inc=True)
                S.matmul(pden[0:1, rs_], P.cbf[:, 384:385], st_[:, r, :], start=True, stop=(c == 0), inc=True)
                if c > 0:
                    S.matmul(pden[0:1, rs_], Cbf[b][:, c, 128:129], bq[b][:, cs], start=False, stop=True, inc=True)

            def epilogue(po=po, pden=pden, G=G):
                k = st["nep"] = st["nep"] + 1
                k %= 2
                S.act(drow[k].v, pden[0:1, :], AF.Square)
                S.act(drow[k].v, drow[k].v, AF.Relu, bias=cm1[:, 0:1])
                S.act(e2row[k].v, drow[k].v, AF.Identity, bias=cm1[:, 1:2], scale=EPS)
                S.act(sqb[k].v, po.v, AF.Square)
                pn = P.psr(7, 8)
                S.matmul(pn.v, P.cbf[:, 256:384], sqb[k].v, start=True, stop=False, inc=True)
                S.matmul(pn.v, P.cbf[0:1, 384:512], e2row[k].v, start=False, stop=True, inc=True)
                S.act(rs[k].v, pn.v, AF.Ln)
                S.act(rs[k].v, rs[k].v, AF.Exp, scale=-0.5)
                S.copy("act", y1[k].v, po.v)

                def epilogue_b():
                    S.tt("dve", y1[k].v, y1[k].v, rs[k].v, ALU.mult)
                    S.stt("dve", yst[k].v, y1[k].v, P.pcol[:, PC_MNG + h:PC_MNG + h + 1], og[b][:, G * GT:(G + 1) * GT],
                          ALU.mult, ALU.mult)
                    S.dma("sp", D_["catT"][512 + h * 128:512 + (h + 1) * 128, G * GT:(G + 1) * GT], yst[k].v)
                return epilogue_b
            return epilogue

        pipeline([{"G": G} for G in range(NG)], stage_a, stage_b, 1, defer=1)

    loads(0)
    pf_issue(after=og[0].last_w + Vt[0].last_w + bk[0].last_w)
    pre_scale(0)
    for q in range(8):
        pre_scan(0, q)
    if "dbgC" in D_:
        S.dma("sp", D_["dbgC"], Cbf[0].ap.rearrange("p a b -> p (a b)") if False else V(Cbf[0], Cbf[0].ap.rearrange("p a b -> p (a b)")))
        S.dma("sp", D_["dbgK"], V(kdT[0], kdT[0].ap.rearrange("p a b -> p (a b)")))
        S.dma("sp", D_["dbgV"], V(Vt[0], Vt[0].ap.rearrange("p a b -> p (a b)")))
        S.dma("sp", D_["dbgE"], ecol[0].v)
        S.dma("sp", D_["dbgQ"], bq[0].v)
    sched = {2: [], 3: [0, 1], 4: [2, 3], 5: [4, 5], 6: [6], 7: [7]}
    for h in range(4):
        hook = None
        if h + 1 < 4:
            loads(h + 1)

            def hook(G, h=h):
                if G == 2:
                    pre_scale(h + 1)
                for q in sched.get(G, []):
                    pre_scan(h + 1, q)
        main(h, hook)
    S.barrier()


def load_w_bf16(P, name, dram_ap, nk, ncols, issue=True):
    tiles = []
    for k in range(nk):
        off = P.A.off
        t = P.A.alloc([128, ncols], BF16, "%s%d" % (name, k))
        pf = P.pref.get((name, k))
        if pf is not None and pf[0] == off:
            tiles.append(pf[1])
            continue
        P.pending_w.append((t, dram_ap[k * 128:(k + 1) * 128, :]))
        tiles.append(t)
    if issue:
        flush_w(P)
    return tiles


def flush_w(P, after=()):
    for t, src in P.pending_w:
        P.S.dma("pool", t.v, src, after=after)
    P.pending_w = []


def prefetch_w(P, specs):
    P.pref = {}
    P.A.off = P.abase
    todo = []
    for name, dram_ap, nk, ncols in specs:
        for k in range(nk):
            off = P.A.off
            t = P.A.alloc([128, ncols], BF16, "%s%d" % (name, k))
            todo.append((t, dram_ap[k * 128:(k + 1) * 128, :]))
            P.pref[(name, k)] = (off, t)

    def issue(after=()):
        for t, src in todo:
            P.S.dma("pool", t.v, src, after=after)
    return issue


def phase_outproj_ffnup(P, layer, h_in, wout_name, wup_name):
    S, A, D_ = P.S, P.A, P.dram
    A.off = P.abase
    Wo = load_w_bf16(P, "wo", D_[wout_name], 8, 1024, issue=False)
    Wu = load_w_bf16(P, "wu", D_[wup_name], 8, 5632, issue=False)
    catg = A.alloc([128, 8, GT], BF16, "catg")
    xg = [A.alloc([128, 8, GT], F32, "xg%d" % i) for i in range(2)]
    y = [A.alloc([128, 8, GT], BF16, "y%d" % i) for i in range(2)]
    ast = [A.alloc([128, 11, GT], BF16, "ast%d" % i) for i in range(2)]
    ga = [A.alloc([128, GT], F32, "ga%d" % i) for i in range(3)]
    va = [A.alloc([128, GT], F32, "va%d" % i) for i in range(3)]
    hv = D_[h_in].rearrange("(kc p) t -> p kc t", p=128)
    h1v = D_["h1T"].rearrange("(kc p) t -> p kc t", p=128)
    cv = D_["catT"].rearrange("(kc p) t -> p kc t", p=128)
    av = D_["aT"].rearrange("(c p) t -> p c t", p=128)
    cwb, cbb = PC_FCW[layer], PC_FCB[layer]
    stt_ = {"n": 0}
    STEP = GT - 2
    ngr = (S_LEN + STEP - 1) // STEP
    rng = []
    for g in range(ngr):
        s0 = g * STEP
        e0 = min(S_LEN, s0 + STEP)
        rng.append((s0, e0, e0 - s0 + 2))

    def pre_(g):
        s0, e0, w = rng[g]
        if g == 0:
            S.memset("dve", catg[:, :, 0:2], 0.0)
            S.memset("dve", xg[0][:, :, 0:2], 0.0)
            S.dma("act", catg[:, :, 2:w], cv[:, :, s0:e0])
            S.dma("act", xg[0][:, :, 2:w], hv[:, :, s0:e0])
        else:
            S.dma("act", catg[:, :, 0:w], cv[:, :, s0 - 2:e0])
            S.dma("act", xg[g % 2][:, :, 0:w], hv[:, :, s0 - 2:e0])

    def norm_(g):
        b = g % 2
        s0, e0, w = rng[g]
        for dc in range(8):
            ps = P.nextps()
            for kc in range(8):
                S.matmul(ps[:, 0:w], Wo[kc][:, dc * 128:(dc + 1) * 128], catg[:, kc, 0:w], start=(kc == 0), stop=(kc == 7))
            S.tt("dve", xg[b][:, dc, 0:w], ps[:, 0:w], xg[b][:, dc, 0:w], ALU.add)
        S.dma("sp", h1v[:, :, s0:e0], xg[b][:, :, 2:w])
        rmsnorm_group(P, xg[b], 1 + 2 * layer, y[b], w=w)

    def body(g, fcs, early=False):
        s0, e0, w = rng[g]
        wn = w - 2
        yb = y[g % 2]
        if early:
            chs = [half * 22 + fc for fc in fcs for half in (0, 1)]
            assert len(chs) == 8
            for kc in range(8):
                for i, ch in enumerate(chs):
                    S.matmul(P.ps[i][:, 0:w], Wu[kc][:, ch * 128:(ch + 1) * 128], yb[:, kc, 0:w], start=(kc == 0),
                             stop=(kc == 7), inc=(kc == 7))
        for fi, fc in enumerate(fcs):
            n = stt_["n"]
            stt_["n"] += 1
            accs = []
            for half, abuf in ((0, ga), (1, va)):
                ch = half * 22 + fc
                if early:
                    ps = P.ps[fi * 2 + half]
                else:
                    ps = P.nextps()
                for kc in range(0 if early else 8):
                    S.matmul(ps[:, 0:w], Wu[kc][:, ch * 128:(ch + 1) * 128], yb[:, kc, 0:w], start=(kc == 0), stop=(kc == 7))
                ac = abuf[n % 3]
                w0 = P.pcol[:, cwb + ch * 3 + 0:cwb + ch * 3 + 1]
                w1 = P.pcol[:, cwb + ch * 3 + 1:cwb + ch * 3 + 2]
                w2 = P.pcol[:, cwb + ch * 3 + 2:cwb + ch * 3 + 3]
                bia = P.pcol[:, cbb + ch:cbb + ch + 1]
                S.act(ac[:, 0:wn], ps[:, 2:w], AF.Identity, bias=bia, scale=w2)
                S.stt("dve", ac[:, 0:wn], ps[:, 1:w - 1], w1, ac[:, 0:wn], ALU.mult, ALU.add)
                S.stt("dve", ac[:, 0:wn], ps[:, 0:wn], w0, ac[:, 0:wn], ALU.mult, ALU.add)
                accs.append(ac)
            S.act(accs[0][:, 0:wn], accs[0][:, 0:wn], AF.Silu)
            a_ = ast[(fc // 11 + 2 * g) % 2]
            S.tt("pool", a_[:, fc % 11, 0:wn], accs[0][:, 0:wn], accs[1][:, 0:wn], ALU.mult)
            if fc % 11 == 10:
                S.dma("sp", av[:, (fc // 11) * 11:(fc // 11) * 11 + 11, s0:e0], a_[:, :, 0:wn])

    pre_(0)
    flush_w(P, after=catg.last_w + xg[0].last_w)
    norm_(0)
    for g in range(ngr):
        if g + 1 < ngr:
            pre_(g + 1)
        if g == 0:
            norm_(1)
            body(0, range(0, 4), early=ARRIVAL_ORDER)
            body(0, range(4, 22))
            continue
        body(g, range(0, 8))
        if g + 1 < ngr and g + 1 != 1:
            norm_(g + 1)
        body(g, range(8, 22))
    S.barrier()


def phase_ffndown(P, layer, wdown_name, final, pf=()):
    S, A, D_ = P.S, P.A, P.dram
    pf_issue = prefetch_w(P, pf)
    Wd = load_w_bf16(P, "wd", D_[wdown_name], 22, 1024, issue=False)
    ag = [A.alloc([128, 22, GT], BF16, "ag%d" % i) for i in range(2)]
    hg = [A.alloc([128, 8, GT], F32, "hg%d" % i) for i in range(2)]
    yo = A.alloc([128, 8, GT], F32, "yo") if final else None
    h1v = D_["h1T"].rearrange("(kc p) t -> p kc t", p=128)
    ov = D_["outT" if final else "hT"].rearrange("(kc p) t -> p kc t", p=128)
    av = D_["aT"].rearrange("(c p) t -> p c t", p=128)
    def pre_(g):
        ts_ = slice(g * GT, (g + 1) * GT)
        S.dma("act", ag[g % 2].v, av[:, :, ts_])
        S.dma("sp", hg[g % 2].v, h1v[:, :, ts_])

    pre_(0)
    flush_w(P, after=ag[0].last_w + hg[0].last_w)
    pf_issue()
    for g in range(NG):
        b = g % 2
        ts_ = slice(g * GT, (g + 1) * GT)
        if g + 1 < NG:
            pre_(g + 1)
        if g == 0 and ARRIVAL_ORDER:
            for fc in range(22):
                for dc in range(8):
                    S.matmul(P.ps[dc].v, Wd[fc][:, dc * 128:(dc + 1) * 128], ag[b][:, fc, :], start=(fc == 0),
                             stop=(fc == 21), inc=(fc == 21))
            for dc in range(8):
                S.tt("dve", hg[b][:, dc, :], P.ps[dc].v, hg[b][:, dc, :], ALU.add)
        else:
            for dc in range(8):
                ps = P.nextps()
                for fc in range(22):
                    S.matmul(ps.v, Wd[fc][:, dc * 128:(dc + 1) * 128], ag[b][:, fc, :], start=(fc == 0), stop=(fc == 21))
                S.tt("dve", hg[b][:, dc, :], ps.v, hg[b][:, dc, :], ALU.add)
        if final:
            rmsnorm_group(P, hg[b], 4, yo)
            S.dma("sp", ov[:, :, ts_], yo.v)
        else:
            S.dma("sp", ov[:, :, ts_], hg[b].v)
    S.barrier()


RET_GAMMA = [1.0 - 2.0 ** (-5.0 - h) for h in range(4)]


def make_cd_consts(inp):
    w = np.asarray(inp["w_in_cd"][0], np.float32)
    qk = w[:, 512:1024].reshape(1024, 8, 2, 32)
    w_rot = np.ascontiguousarray(qk[:, :, ::-1, :].reshape(1024, 512))
    half = 32
    inv = (np.float32(10000.0) ** (-np.arange(half, dtype=np.float32) / np.float32(half))).astype(np.float32)
    ang = (np.arange(S_LEN, dtype=np.float32)[:, None] * inv[None, :]).astype(np.float32).astype(np.float64)
    cos, sin = np.cos(ang).T, np.sin(ang).T
    t = np.arange(S_LEN, dtype=np.float64)
    rtab = np.zeros((4, 128, 2, S_LEN), np.float64)
    for c2 in range(4):
        for p in range(128):
            hd = (c2 % 2) * 2 + p // 64
            pp = p % 64
            f = pp % 32
            lg = math.log(RET_GAMMA[hd])
            if c2 < 2:
                dfac = np.exp(lg * ((t % 128) + 1))
            else:
                dfac = np.exp(-lg * ((t % 128) + 1)) * (64.0 ** -0.5)
            rtab[c2, p, 0] = cos[f] * dfac
            rtab[c2, p, 1] = (-sin[f] if pp < 32 else sin[f]) * dfac
    invc = np.zeros((128, 4, 16), np.float32)
    for c, wdw in enumerate((2, 4, 8, 16)):
        invc[:, c, :] = (1.0 / np.minimum(np.arange(16) + 1, wdw)).astype(np.float32)[None, :]
    return {"w_in_cd": np.ascontiguousarray(w), "w_rot": w_rot, "rtab": rtab.astype(np.float32).reshape(4 * 128, 2 * S_LEN),
            "invc": invc.reshape(128, 64), "pool_w": np.ascontiguousarray(np.asarray(inp["pool_w"][0], np.float32).reshape(512, 128))}


def phase_proj_cd(P):
    S, A, D_ = P.S, P.A, P.dram
    A.off = P.abase
    Wk = load_w_bf16(P, "wcd", D_["w_in_cd"], 8, 2048)
    Wr = load_w_bf16(P, "wrot", D_["w_rot"], 8, 512)
    pw = A.alloc([128, 4, 128], BF16, "pw")
    S.dma("pool", pw.v, D_["pool_w"].rearrange("(g c) d -> c g d", g=4))
    invc = A.alloc([128, 4, 16], F32, "invc")
    S.dma("sp", invc.v, D_["invc"].rearrange("p (a b) -> p a b", a=4))
    xg = [A.alloc([128, 8, GT], F32, "xg%d" % i) for i in range(2)]
    y = [A.alloc([128, 8, GT], BF16, "y%d" % i) for i in range(2)]
    tab = [A.alloc([128, 2, GT], F32, "tab%d" % i) for i in range(2)]
    pre2 = [A.alloc([128, 16 + GT], F32, "pre%d" % i) for i in range(2)]
    sa2 = [A.alloc([128, 16 + GT], F32, "sa%d" % i) for i in range(2)]
    sb2 = [A.alloc([128, 16 + GT], F32, "sb%d" % i) for i in range(2)]
    tmp16 = A.alloc([128, 16], F32, "tmp16")
    pooled = [A.alloc([128, GT], BF16, "pooled%d" % i) for i in range(8)]
    cst_ = [A.alloc([128, GT], BF16, "cst%d" % i) for i in range(2)]
    t1 = A.alloc([128, GT], F32, "t1")
    t2 = A.alloc([128, GT], F32, "t2")
    rqst = [A.alloc([128, GT], BF16, "rqst%d" % i) for i in range(2)]
    rvst = [A.alloc([128, 4, 512], BF16, "rvst%d" % i) for i in range(2)]
    rgst = [A.alloc([128, 4, GT], BF16, "rgst%d" % i) for i in range(2)]
    tail = [A.alloc([128, 16], F32, "ptail%d" % c) for c in range(4)]
    for c in range(4):
        S.memset("dve", tail[c].v, 0.0)
    hv = D_["hT"].rearrange("(kc p) t -> p kc t", p=128)
    rqkv = D_["rqk"].rearrange("(c two) p t -> c (two p) t", two=2)
    rtv = D_["rtab"].rearrange("(c p) (a t) -> c p a t", p=128, a=2)
    stt_ = {"n": 0}
    mixq = []

    def pre_(g):
        S.dma("act", xg[g % 2].v, hv[:, :, g * GT:(g + 1) * GT])

    def norm_(g):
        rmsnorm_group(P, xg[g % 2], 2, y[g % 2])

    def fm_chunk(yb, W, col0):
        ps = P.nextps()
        for kc in range(8):
            S.matmul(ps.v, W[kc][:, col0:col0 + 128], yb[:, kc, :], start=(kc == 0), stop=(kc == 7))
        return ps

    def body1(g):
        b = g % 2
        ts_ = slice(g * GT, (g + 1) * GT)
        yb = y[b]
        for c in range(4):
            wdw = 2 ** (c + 1)
            ps = fm_chunk(yb, Wk, c * 128)
            pre, sa, sb = pre2[c % 2], sa2[c % 2], sb2[c % 2]
            S.copy("act", pre[:, 16:16 + GT], ps.v)
            S.copy("pool", pre[:, 0:16], tail[c].v)
            S.copy("pool", tail[c].v, pre[:, GT:GT + 16])
            src = pre
            for k in range(c + 1):
                sh = 2 ** k
                dst = sa if src is not sa else sb
                S.tt("dve" if k % 2 == 0 else "pool", dst[:, sh:16 + GT], src[:, sh:16 + GT], src[:, 0:16 + GT - sh], ALU.add)
                src = dst
            n = stt_["n"]
            stt_["n"] += 1
            pl = pooled[n % 8]
            S.stt("dve", pl.v, src[:, 16:16 + GT], 1.0 / wdw, pre[:, 16:16 + GT], ALU.mult, ALU.subtract)
            if g == 0:
                S.tt("dve", tmp16.v, src[:, 16:32], invc[:, c, :], ALU.mult)
                S.tt("dve", pl[:, 0:16], tmp16.v, pre[:, 16:32], ALU.subtract)
            mixq.append((c, pl, ts_))
        for c2 in range(4):
            ps = fm_chunk(yb, Wk, 512 + c2 * 128)
            psr_ = fm_chunk(yb, Wr, c2 * 128)
            tb_ = tab[c2 % 2]
            S.dma("act", tb_.v, rtv[c2][:, :, ts_])
            S.tt("dve", t1.v, ps.v, tb_[:, 0, :], ALU.mult)
            S.tt("dve", t2.v, psr_.v, tb_[:, 1, :], ALU.mult)
            rq_ = rqst[c2 % 2]
            S.tt("pool", rq_.v, t1.v, t2.v, ALU.add)
            S.dma("sp", rqkv[c2][:, ts_], rq_.v)

    def body2(g):
        b = g % 2
        ts_ = slice(g * GT, (g + 1) * GT)
        yb = y[b]
        for sub in range(4):
            ps = P.nextps()
            for kc in range(8):
                S.matmul(ps.v, yb[:, kc, sub * 128:(sub + 1) * 128], Wk[kc][:, 1024:1536], start=(kc == 0), stop=(kc == 7))
            S.evac(rvst[b][:, sub, :], ps.v)
        S.dma("sp", D_["rv"].rearrange("(n p) f -> p n f", p=128)[:, 4 * g:4 * g + 4, :], rvst[b].v)
        for c in range(4):
            ps = fm_chunk(yb, Wk, 1536 + c * 128)
            S.act(rgst[b][:, c, :], ps.v, AF.Silu)
        S.dma("sp", D_["rg"].rearrange("c p t -> p c t")[:, :, ts_], rgst[b].v)
        while mixq:
            c, pl, tsl = mixq.pop(0)
            psm = P.nextps()
            S.matmul(psm.v, pw[:, c, :], pl.v)
            cs = cst_[c % 2]
            S.ts("dve", cs.v, psm.v, P.pcol[:, PC_PSC + c:PC_PSC + c + 1], None, ALU.mult)
            S.dma("sp", D_["catT"][c * 128:(c + 1) * 128, tsl], cs.v)

    pre_(0)
    norm_(0)
    for g in range(NG):
        if g + 1 < NG:
            pre_(g + 1)
        body1(g)
        if g + 1 < NG:
            norm_(g + 1)
        body2(g)
    S.barrier()


def phase_retention(P, pf=()):
    S, A, D_ = P.S, P.A, P.dram
    pf_issue = prefetch_w(P, pf)
    NCH = 32
    Qd = [A.alloc([64, S_LEN], BF16, "Qd%d" % i) for i in range(2)]
    Kd = [A.alloc([64, S_LEN], BF16, "Kd%d" % i) for i in range(2)]
    sg = [A.alloc([128, S_LEN], BF16, "sg%d" % i) for i in range(2)]
    Vt = [A.alloc([128, NCH, 128], BF16, "rVt%d" % i) for i in range(2)]
    KdT = [A.alloc([128, NCH, 64], BF16, "KdT%d" % i) for i in range(2)]
    Rbf = [A.alloc([64, NCH, 128], BF16, "Rbf%d" % i) for i in range(2)]
    T32 = [[A.alloc([64, 128], F32, "T32%d_%d" % (i, k)) for k in range(2)] for i in range(2)]
    c01 = P.c01
    St = [A.alloc([128, 4, 128], BF16, "St%d" % i) for i in range(3)]
    sqb = [A.alloc([128, GT], BF16, "sqb%d" % i) for i in range(2)]
    rs = [A.alloc([128, GT], F32, "rs%d" % i) for i in range(2)]
    y1 = [A.alloc([128, GT], F32, "y1%d" % i) for i in range(2)]
    yst = [A.alloc([128, GT], BF16, "yst%d" % i) for i in range(2)]
    rvv = D_["rv"].rearrange("(n p) f -> p n f", p=128)
    ident = P.cbf[0:64, 0:64]
    st = {"nep": 0, "nst": 0}

    def loads(h):
        b = h % 2
        S.dma("sp", Kd[b].v, D_["rqk"][4 + h])
        S.dma("sp", Vt[b].v, rvv[:, :, h * 128:(h + 1) * 128])
        S.dma("sp", Qd[b].v, D_["rqk"][h])
        S.dma("sp", sg[b].v, D_["rg"][h])

    def pre_transposes(h):
        b = h % 2
        for q in range(4):
            pst = P.psr(6, 7)
            pstb = V(pst, pst.ap.bitcast(BF16))
            for r in range(8):
                c = q * 8 + r
                S.transpose(pstb[:, r * 64:(r + 1) * 64], Kd[b][0:64, c * 128:(c + 1) * 128], ident)
            S.evac(KdT[b][:, q * 8:(q + 1) * 8, :], V(pst, pst.ap.bitcast(BF16)[:, 0:512].rearrange("p (a b) -> p a b", a=8)))

    def pre_scan(h, q):
        b = h % 2
        g128 = RET_GAMMA[h] ** 128
        psd = P.psr(4, 6)
        for r in range(4):
            c = q * 4 + r
            if c < NCH - 1:
                S.matmul(psd[0:64, r * 128:(r + 1) * 128], KdT[b][:, c, :], Vt[b][:, c, :])
        for r in range(4):
            c = q * 4 + r
            if c == NCH - 1:
                continue
            tn, tp = T32[b][c % 2], T32[b][(c + 1) % 2]
            if c == 0:
                S.copy("dve", tn.v, psd[0:64, 0:128])
            else:
                S.stt("dve", tn.v, tp.v, g128, psd[0:64, r * 128:(r + 1) * 128], ALU.mult, ALU.add)
            S.ts("dve", Rbf[b][:, c + 1, :], tn.v, g128, None, ALU.mult)

    def main(h, hook):
        b = h % 2

        def stage_a(it):
            G = it["G"]
            pss = P.psr(2, 4)
            for r in range(4):
                c = 4 * G + r
                cs = slice(c * 128, (c + 1) * 128)
                S.matmul(pss[:, r * 128:(r + 1) * 128], Kd[b][:, cs], Qd[b][:, cs])
            st_ = St[st["nst"] % 3]
            st["nst"] += 1
            S.tt("dve", st_.v, V(pss, pss.ap.rearrange("p (a b) -> p a b", a=4)),
                 V(c01, c01.ap.unsqueeze(1).broadcast_to([128, 4, 128])), ALU.mult)
            it["st"] = st_
            if hook is not None:
                hook(G)

        def stage_b(it):
            G, st_ = it["G"], it["st"]
            po = P.psr(0, 2)
            for r in range(4):
                c = 4 * G + r
                cs = slice(c * 128, (c + 1) * 128)
                S.matmul(po[:, r * 128:(r + 1) * 128], Vt[b][:, c, :], st_[:, r, :], start=True, stop=(c == 0), inc=True)
                if c > 0:
                    S.matmul(po[:, r * 128:(r + 1) * 128], Rbf[b][:, c, :], Qd[b][:, cs], start=False, stop=True, inc=True)

            def epilogue(po=po, G=G):
                k = st["nep"] = st["nep"] + 1
                k %= 2
                S.act(sqb[k].v, po.v, AF.Square)
                pn = P.psr(7, 8)
                S.matmul(pn.v, P.cbf[:, 256:384], sqb[k].v)
                S.act(rs[k].v, pn.v, AF.Ln, bias=P.epsc[:, 0:1])
                S.act(rs[k].v, rs[k].v, AF.Exp, scale=-0.5)
                S.copy("act", y1[k].v, po.v)

                def epilogue_b():
                    S.tt("dve", y1[k].v, y1[k].v, rs[k].v, ALU.mult)
                    S.stt("dve", yst[k].v, y1[k].v, P.pcol[:, PC_RNG + h:PC_RNG + h + 1], sg[b][:, G * GT:(G + 1) * GT],
                          ALU.mult, ALU.mult)
                    S.dma("sp", D_["catT"][512 + h * 128:512 + (h + 1) * 128, G * GT:(G + 1) * GT], yst[k].v)
                return epilogue_b
            return epilogue

        pipeline([{"G": G} for G in range(NG)], stage_a, stage_b, 1, defer=1)

    loads(0)
    pf_issue(after=sg[0].last_w + Vt[0].last_w + Qd[0].last_w)
    pre_transposes(0)
    for q in range(8):
        pre_scan(0, q)
    sched = {2: [], 3: [0, 1], 4: [2, 3], 5: [4, 5], 6: [6], 7: [7]}
    for h in range(4):
        hook = None
        if h + 1 < 4:
            loads(h + 1)

            def hook(G, h=h):
                if G == 2:
                    pre_transposes(h + 1)
                for q in sched.get(G, []):
                    pre_scan(h + 1, q)
        main(h, hook)
    S.barrier()


_NC_CACHE = {}


def kernel(**inputs):
    inp = {k: np.asarray(v) for k, v in inputs.items()}
    if "nc" not in _NC_CACHE:
        _NC_CACHE["nc"] = build_program()
    nc = _NC_CACHE["nc"]
    shared = make_inputs(inp, 0)
    in_maps = []
    for b in range(8):
        m = dict(shared)
        m["xT"] = np.ascontiguousarray(inp["x"][b].T)
        in_maps.append(m)
    res = run_bass_kernel_spmd(nc, in_maps, core_ids=list(range(8)))
    out = np.stack([np.asarray(res.results[b]["outT"], np.float32).T for b in range(8)], axis=0)
    return np.ascontiguousarray(out.astype(np.float32))
```
